# Optimizing a Trainium2 kernel written in Bass

```python
import jax, jax.numpy as jnp
from jax import lax
import numpy as np

D_MODEL = 1024
BATCH = 2
SEQ = 16384
DEPTH = 2
DEC_BATCH = 16
DEC_SEQ = 16
PAST_LEN = 1024

CHUNK = 64
GDN_HEADS = 4
GDN_HEAD_DIM = 128
GDN_WIDTH = GDN_HEADS * GDN_HEAD_DIM
SC_WIDTH = D_MODEL - GDN_WIDTH
GDN_CONV = 4
SC_CONV = 3
D_FF = -(-8 * D_MODEL // (3 * 256)) * 256
EPS = 1e-6
OFF_QKV = 3 * GDN_WIDTH
OFF_Z = OFF_QKV + GDN_WIDTH
OFF_B = OFF_Z + GDN_HEADS
OFF_A = OFF_B + GDN_HEADS
OFF_SB = OFF_A + SC_WIDTH
OFF_SC = OFF_SB + SC_WIDTH
IN_DIM = OFF_SC + SC_WIDTH

kernel_name = 'hybrid_gdn_shortconv_stream_step'


def rmsnorm(x, g):
    xf = x.astype(jnp.float32)
    y = xf * lax.rsqrt(jnp.mean(xf * xf, axis=-1, keepdims=True) + EPS)
    return (y * g.astype(jnp.float32)).astype(x.dtype)


def l2norm(x):
    return x * lax.rsqrt(jnp.sum(x * x, axis=-1, keepdims=True) + EPS)


def causal_dwconv(x, prev, w):
    width = w.shape[0]
    L = x.shape[1]
    xp = jnp.concatenate([prev.astype(x.dtype), x], axis=1)
    y = xp[:, 0:L] * w[0]
    for i in range(1, width):
        y = y + xp[:, i:i + L] * w[i]
    return y, xp[:, xp.shape[1] - (width - 1):]


def gated_delta_rule(q, k, v, g, beta, S0, chunk):
    B, L, H, DK = q.shape
    DV = v.shape[-1]
    N = L // chunk

    def blk(t):
        return jnp.moveaxis(t.reshape((B, N, chunk, H) + t.shape[3:]), 3, 2)

    q, k, v, beta = blk(q), blk(k), blk(v), blk(beta)
    g = jnp.cumsum(blk(g), axis=-1)
    idx = jnp.arange(chunk)
    causal = idx[:, None] >= idx[None, :]
    strict = idx[:, None] > idx[None, :]
    decay = jnp.exp(jnp.where(causal, g[..., :, None] - g[..., None, :], -jnp.inf))
    kb = k * beta[..., None]
    A = jnp.where(strict, jnp.einsum('bnhid,bnhjd->bnhij', kb, k) * decay, 0.0)
    eye = jnp.eye(chunk, dtype=jnp.float32)
    rhs = jnp.concatenate([v * beta[..., None], kb * jnp.exp(g)[..., None]], axis=-1)
    sol = lax.linalg.triangular_solve(eye + A, rhs, left_side=True, lower=True, unit_diagonal=True)
    u, w = sol[..., :DV], sol[..., DV:]
    qk = jnp.einsum('bnhid,bnhjd->bnhij', q, k) * decay
    qg = q * jnp.exp(g)[..., None]
    g_last = g[..., -1]
    k_tail = k * jnp.exp(g_last[..., None] - g)[..., None]
    a_last = jnp.exp(g_last)

    def step(S, xs):
        qg_c, qk_c, u_c, w_c, kt_c, al_c = xs
        v_new = u_c - jnp.einsum('bhcd,bhde->bhce', w_c, S)
        o = jnp.einsum('bhcd,bhde->bhce', qg_c, S) + jnp.einsum('bhij,bhje->bhie', qk_c, v_new)
        S = S * al_c[..., None, None] + jnp.einsum('bhcd,bhce->bhde', kt_c, v_new)
        return S, o

    xs = tuple(jnp.moveaxis(t, 1, 0) for t in (qg, qk, u, w, k_tail, a_last))
    S, o = lax.scan(step, S0.astype(jnp.float32), xs)
    o = o.transpose(1, 0, 3, 2, 4).reshape(B, L, H, DV)
    return o, S


def hybrid_layer(x, conv_prev, S0, sc_prev, norm_mix_pre, w_in, conv_qkv_w, a_log, dt_bias,
                 gdn_norm_w, conv_sc_w, w_o, norm_mix_post, norm_ffn_pre, w_gate, w_up, w_down,
                 norm_ffn_post):
    B, L, _ = x.shape
    chunk = min(CHUNK, L)
    h = rmsnorm(x, norm_mix_pre)
    P = h @ w_in
    qkv_in = P[..., :OFF_QKV]
    z = P[..., OFF_QKV:OFF_Z]
    b_raw = P[..., OFF_Z:OFF_B].astype(jnp.float32)
    a_raw = P[..., OFF_B:OFF_A].astype(jnp.float32)
    sc_b = P[..., OFF_A:OFF_SB]
    sc_c = P[..., OFF_SB:OFF_SC]
    sc_h = P[..., OFF_SC:]

    qkv, conv_new = causal_dwconv(qkv_in, conv_prev, conv_qkv_w)
    qkv = jax.nn.silu(qkv.astype(jnp.float32))
    q = l2norm(qkv[..., :GDN_WIDTH].reshape(B, L, GDN_HEADS, GDN_HEAD_DIM)) * (GDN_HEAD_DIM ** -0.5)
    k = l2norm(qkv[..., GDN_WIDTH:2 * GDN_WIDTH].reshape(B, L, GDN_HEADS, GDN_HEAD_DIM))
    v = qkv[..., 2 * GDN_WIDTH:].reshape(B, L, GDN_HEADS, GDN_HEAD_DIM)
    beta = jax.nn.sigmoid(b_raw)
    g = -jnp.exp(a_log.astype(jnp.float32)) * jax.nn.softplus(a_raw + dt_bias.astype(jnp.float32))
    o, S_new = gated_delta_rule(q, k, v, g, beta, S0, chunk)
    zf = z.astype(jnp.float32).reshape(B, L, GDN_HEADS, GDN_HEAD_DIM)
    o = (rmsnorm(o, gdn_norm_w) * jax.nn.silu(zf)).reshape(B, L, GDN_WIDTH).astype(x.dtype)

    sc_y, sc_new = causal_dwconv(sc_c * sc_h, sc_prev, conv_sc_w)
    sc_out = sc_b * sc_y

    mix = jnp.concatenate([o, sc_out.astype(x.dtype)], axis=-1) @ w_o
    x = x + rmsnorm(mix, norm_mix_post)

    h = rmsnorm(x, norm_ffn_pre)
    f = (jax.nn.silu(h @ w_gate) * (h @ w_up)) @ w_down
    x = x + rmsnorm(f, norm_ffn_post)
    return x, conv_new, S_new, sc_new


def setup_inputs(seed: int = 0) -> dict:
    key = jax.random.key(seed)
    ks = jax.random.split(key, 24)
    f32 = jnp.float32

    def nrm(k, shape, scale):
        return jax.random.normal(k, shape, f32) * scale

    def gain(k, shape):
        return 1.0 + 0.02 * jax.random.normal(k, shape, f32)

    dt = jnp.exp(jax.random.uniform(ks[9], (DEPTH, GDN_HEADS), f32, np.log(1e-3), np.log(1e-1)))
    return {
        'x_prompt': nrm(ks[0], (BATCH, SEQ, D_MODEL), 1.0),
        'x_sample': nrm(ks[1], (DEC_BATCH, DEC_SEQ, D_MODEL), 1.0),
        'cache_gdn_conv': nrm(ks[2], (DEPTH, DEC_BATCH, GDN_CONV - 1, 3 * GDN_WIDTH), 1.0),
        'state_gdn': nrm(ks[3], (DEPTH, DEC_BATCH, GDN_HEADS, GDN_HEAD_DIM, GDN_HEAD_DIM), 0.05),
        'cache_sc_conv': nrm(ks[4], (DEPTH, DEC_BATCH, SC_CONV - 1, SC_WIDTH), 1.0),
        'norm_mix_pre': gain(ks[5], (DEPTH, D_MODEL)),
        'w_in': nrm(ks[6], (DEPTH, D_MODEL, IN_DIM), D_MODEL ** -0.5),
        'conv_qkv_w': nrm(ks[7], (DEPTH, GDN_CONV, 3 * GDN_WIDTH), GDN_CONV ** -0.5),
        'a_log': jnp.log(jax.random.uniform(ks[8], (DEPTH, GDN_HEADS), f32, 1.0, 16.0)),
        'dt_bias': dt + jnp.log(-jnp.expm1(-dt)),
        'gdn_norm_w': gain(ks[10], (DEPTH, GDN_HEAD_DIM)),
        'conv_sc_w': nrm(ks[11], (DEPTH, SC_CONV, SC_WIDTH), SC_CONV ** -0.5),
        'w_o': nrm(ks[12], (DEPTH, D_MODEL, D_MODEL), D_MODEL ** -0.5),
        'norm_mix_post': gain(ks[13], (DEPTH, D_MODEL)),
        'norm_ffn_pre': gain(ks[14], (DEPTH, D_MODEL)),
        'w_gate': nrm(ks[15], (DEPTH, D_MODEL, D_FF), D_MODEL ** -0.5),
        'w_up': nrm(ks[16], (DEPTH, D_MODEL, D_FF), D_MODEL ** -0.5),
        'w_down': nrm(ks[17], (DEPTH, D_FF, D_MODEL), D_FF ** -0.5),
        'norm_ffn_post': gain(ks[18], (DEPTH, D_MODEL)),
    }


def reference(x_prompt, x_sample, cache_gdn_conv, state_gdn, cache_sc_conv, norm_mix_pre, w_in,
              conv_qkv_w, a_log, dt_bias, gdn_norm_w, conv_sc_w, w_o, norm_mix_post, norm_ffn_pre,
              w_gate, w_up, w_down, norm_ffn_post):
    params = (norm_mix_pre, w_in, conv_qkv_w, a_log, dt_bias, gdn_norm_w, conv_sc_w, w_o,
              norm_mix_post, norm_ffn_pre, w_gate, w_up, w_down, norm_ffn_post)

    def run(x, conv0, s0, sc0):
        convs, states, scs = [], [], []
        for l in range(DEPTH):
            x, c, s, sc = hybrid_layer(x, conv0[l], s0[l], sc0[l], *[p[l] for p in params])
            convs.append(c)
            states.append(s)
            scs.append(sc)
        return x, jnp.stack(convs), jnp.stack(states), jnp.stack(scs)

    Bp = x_prompt.shape[0]
    zc = jnp.zeros((DEPTH, Bp, GDN_CONV - 1, 3 * GDN_WIDTH), x_prompt.dtype)
    zs = jnp.zeros((DEPTH, Bp, GDN_HEADS, GDN_HEAD_DIM, GDN_HEAD_DIM), jnp.float32)
    zsc = jnp.zeros((DEPTH, Bp, SC_CONV - 1, SC_WIDTH), x_prompt.dtype)
    y_prompt, conv_p, state_p, sc_p = run(x_prompt, zc, zs, zsc)
    y_sample, conv_s, state_s, sc_s = run(x_sample, cache_gdn_conv, state_gdn, cache_sc_conv)
    return (y_prompt, y_sample, conv_p, state_p, sc_p, conv_s, state_s, sc_s)
```

```python
from contextlib import ExitStack
import numpy as np
import concourse.bass as bass
import concourse.mybir as mybir
from concourse.bass_utils import run_bass_kernel_spmd

F32 = mybir.dt.float32
BF16 = mybir.dt.bfloat16
AF = mybir.ActivationFunctionType
ALU = mybir.AluOpType

DM = 1024
DEPTH = 2
DFF = 2816
EPS = 1e-6
BIG = 30000.0


class Prog:
    ENGS = ("pe", "act", "dve", "pool", "sp")

    def __init__(self, nc, n_dma_sems=8):
        self.nc = nc
        self.ops = []
        self.n_dma_sems = n_dma_sems

    def op(self, eng, fn, reads=(), writes=(), dma=False, cc=False):
        self.ops.append(dict(eng=eng, fn=fn, reads=tuple(reads), writes=tuple(writes), dma=dma or cc, cc=cc))

    def finalize(self):
        nc = self.nc
        ops = self.ops
        last_w = {}
        readers = {}
        dma_rr = {e: 0 for e in self.ENGS}
        dma_last = {}
        for i, o in enumerate(ops):
            deps = set()
            for r in o["reads"]:
                if r in last_w:
                    deps.add(last_w[r])
            for w in o["writes"]:
                if w in last_w:
                    deps.add(last_w[w])
                for rd in readers.get(w, ()):
                    deps.add(rd)
            if o["dma"]:
                if o["cc"]:
                    k = ("cc", 0)
                else:
                    k = (o["eng"], dma_rr[o["eng"]] % self.n_dma_sems)
                    dma_rr[o["eng"]] += 1
                o["dsem"] = k
                if k in dma_last:
                    deps.add(dma_last[k])
                dma_last[k] = i
            deps.discard(i)
            if o["eng"] == "pe" and not o["dma"]:
                deps = {d for d in deps if not (ops[d]["eng"] == "pe" and not ops[d]["dma"])}
            best = {}
            keep = set()
            for d in deps:
                if ops[d]["dma"]:
                    keep.add(d)
                else:
                    e_ = ops[d]["eng"]
                    if e_ not in best or best[e_] < d:
                        best[e_] = d
            keep.update(best.values())
            deps = keep
            o["deps"] = deps
            for w in o["writes"]:
                last_w[w] = i
                readers[w] = []
            for r in o["reads"]:
                if r not in o["writes"]:
                    readers.setdefault(r, []).append(i)
        for o in ops:
            o["sig"] = o["dma"]
        for o in ops:
            for d in o["deps"]:
                ops[d]["sig"] = True
        cnt = {e: 0 for e in self.ENGS}
        dcnt = {}
        for o in ops:
            if o["dma"]:
                dcnt[o["dsem"]] = dcnt.get(o["dsem"], 0) + (1 if o["cc"] else 16)
                o["ev"] = (("d",) + o["dsem"], dcnt[o["dsem"]])
            elif o["sig"]:
                cnt[o["eng"]] += 1
                o["ev"] = (("e", o["eng"]), cnt[o["eng"]])
        self.max_counts = dict(cnt)
        with ExitStack() as st:
            sems = {}
            for e in self.ENGS:
                sems[("e", e)] = st.enter_context(nc.semaphore("S_" + e))
            for k in dcnt:
                sems[("d",) + k] = st.enter_context(nc.semaphore("D_%s%d" % k))
            block = st.enter_context(nc.Block())
            final = {}
            for o in ops:
                if o["dma"]:
                    final[o["ev"][0]] = o["ev"][1]

            def run(ename):
                def body(eng):
                    waited = {}
                    for o in ops:
                        if o["eng"] != ename:
                            continue
                        need = {}
                        for d in o["deps"]:
                            s, v = ops[d]["ev"]
                            if need.get(s, 0) < v:
                                need[s] = v
                        for s, v in need.items():
                            if waited.get(s, 0) < v:
                                eng.wait_ge(sems[s], v)
                                waited[s] = v
                        ins = o["fn"](eng)
                        if o["sig"]:
                            ins.then_inc(sems[o["ev"][0]], 16 if (o["dma"] and not o["cc"]) else 1)
                    if ename == "sp":
                        for s, v in final.items():
                            if waited.get(s, 0) < v:
                                eng.wait_ge(sems[s], v)
                return body

            block.tensor(run("pe"))
            block.scalar(run("act"))
            block.vector(run("dve"))
            block.gpsimd(run("pool"))
            block.sync(run("sp"))


def layer_slabs():
    s = []
    for i in range(7):
        s.append(("in", i, 8, 512))
    for i in range(2):
        s.append(("o", i, 8, 512))
    for i in range(6):
        w = 512 if i < 5 else 256
        s.append(("g", i, 8, w))
        s.append(("u", i, 8, w))
    for hf in range(2):
        for q in range(3):
            s.append(("d", hf * 3 + q, 8 if q < 2 else 6, 512))
    return s


import os
DBG = []
STOP = os.environ.get("KSTOP", "")
CFG = {"chain": BF16, "state": BF16}


def build(ntp, with_sample=True, dbg=False):
    nc = bass.Bass("TRN2", target_bir_lowering=False)
    dbg_n = [0]

    def dump(tag, ap, keys, dt=F32):
        if not dbg:
            return
        nm = "dbg%d_%s" % (dbg_n[0], tag)
        dbg_n[0] += 1
        d = nc.dram_tensor(nm, list(ap.shape), dt, kind="ExternalOutput").ap()
        P.op("pool", lambda e: e.dma_start(out=d, in_=ap), keys, (), dma=True)
        DBG.append(nm)

    LP = ntp * 512

    def din(name, shape, dt=F32):
        return nc.dram_tensor(name, list(shape), dt, kind="ExternalInput").ap()

    def dout(name, shape, dt=F32):
        return nc.dram_tensor(name, list(shape), dt, kind="ExternalOutput").ap()

    def dscr(name, shape, dt):
        return nc.dram_tensor(name, list(shape), dt).ap()

    NRANK = 8
    xp = din("xp", [LP, DM])
    xh_in = din("xh", [3, DM])
    mask_d = din("mask", [128, 16])
    x1s = dscr("x1s", [LP, DM], F32)
    cc_src = dscr("cc_src", [128, 1024], F32)
    cc_dst = dscr("cc_dst", [8 * 128, 1024], F32)
    cc2_src = dscr("cc2_src", [3, DM], F32)
    cc2_dst = dscr("cc2_dst", [24, DM], F32)
    tts = dscr("tts", [DEPTH, ntp, 4, 128, 512], BF16)
    xs = din("xs", [32, DM])
    cg = din("cg", [DEPTH, 2, 3, 1536])
    stin = din("stin", [DEPTH, 2, 4, 128, 128])
    cs = din("cs", [DEPTH, 2, 2, 512])
    w_in = din("w_in", [DEPTH, DM, 3584])
    w_ba = din("w_ba", [DEPTH, DM, 8])
    w_o = din("w_o", [DEPTH, DM, DM])
    w_g = din("w_g", [DEPTH, DM, DFF])
    w_u = din("w_u", [DEPTH, DM, DFF])
    w_d = din("w_d", [DEPTH, DFF, DM])
    gpre_d = din("gpre", [128, DEPTH, 2, 8])
    gpost_d = din("gpost", [DEPTH, 2, DM])
    convw_d = din("convw", [128, DEPTH, 12, 4])
    convsc_d = din("convsc", [128, DEPTH, 4, 3])
    alog_d = din("alog", [DEPTH, 4])
    dtb_d = din("dtb", [DEPTH, 4])
    gnw_d = din("gnw", [DEPTH, 128])

    yp = dout("yp", [LP, DM])
    ys = dout("ys", [32, DM])
    convp_o = dout("convp", [DEPTH, 3, 1536])
    statep_o = dout("statep", [DEPTH, 4, 128, 128])
    scp_o = dout("scp", [DEPTH, 2, 512])
    convs_o = dout("convs", [DEPTH, 2, 3, 1536])
    states_o = dout("states", [DEPTH, 2, 4, 128, 128])
    scs_o = dout("scs", [DEPTH, 2, 2, 512])

    sc_in = dscr("sc_in", [DEPTH, 7, 128, 4096], BF16)
    sc_o = dscr("sc_o", [DEPTH, 2, 128, 4096], BF16)
    sc_g = dscr("sc_g", [DEPTH, 6, 128, 4096], BF16)
    sc_u = dscr("sc_u", [DEPTH, 6, 128, 4096], BF16)
    sc_d = dscr("sc_d", [DEPTH, 6, 128, 4096], BF16)
    sc_ba = dscr("sc_ba", [DEPTH, 128, 64], BF16)

    P = Prog(nc)
    st = ExitStack()

    def sb(name, shape, dt=F32):
        return st.enter_context(nc.sbuf_tensor(name, list(shape), dt))

    def E(eng, method, *args, r=(), w=(), **kw):
        P.op(eng, lambda e: getattr(e, method)(*args, **kw), r, w)

    def DMA(q, out, in_, r=(), w=(), **kw):
        P.op(q, lambda e: e.dma_start(out=out, in_=in_, **kw), r, w, dma=True)

    ident = sb("ident", [128, 128])
    identb = sb("identb", [128, 128], BF16)
    Umat = sb("Umat", [128, 128])
    ones = sb("ones", [128, 128])
    onesb = sb("onesb", [128, 128], BF16)
    mstrict = sb("mstrict", [128, 128])
    minclT = sb("minclT", [128, 128])
    E("pool", "memset", ident[:], 1.0, w=["ident"])
    E("pool", "affine_select", out=ident[:], in_=ident[:], pattern=[[-1, 128]], compare_op=ALU.is_equal,
      fill=0.0, base=0, channel_multiplier=1, r=["ident"], w=["ident"])
    E("dve", "tensor_copy", identb[:], ident[:], r=["ident"], w=["identb"])
    E("pool", "memset", Umat[:], 1.0, w=["Umat"])
    E("pool", "affine_select", out=Umat[:], in_=Umat[:], pattern=[[1, 128]], compare_op=ALU.is_ge,
      fill=0.0, base=0, channel_multiplier=-1, r=["Umat"], w=["Umat"])
    E("pool", "memset", ones[:], 1.0, w=["ones"])
    E("pool", "memset", onesb[:], 1.0, w=["onesb"])
    E("pool", "memset", mstrict[:], 0.0, w=["mstrict"])
    E("pool", "affine_select", out=mstrict[:], in_=mstrict[:], pattern=[[-1, 128]], compare_op=ALU.is_gt,
      fill=BIG, base=0, channel_multiplier=1, r=["mstrict"], w=["mstrict"])
    E("pool", "memset", minclT[:], 0.0, w=["minclT"])
    E("pool", "affine_select", out=minclT[:], in_=minclT[:], pattern=[[1, 128]], compare_op=ALU.is_ge,
      fill=BIG, base=0, channel_multiplier=-1, r=["minclT"], w=["minclT"])

    gpre = sb("gpre_s", [128, DEPTH, 2, 8])
    convw = sb("convw_s", [128, DEPTH, 12, 4])
    convsc = sb("convsc_s", [128, DEPTH, 4, 3])
    negA = sb("negA", [128, DEPTH, 4])
    dtb = sb("dtb_s", [128, DEPTH, 4])
    gnw = sb("gnw_s", [128, DEPTH, 128])
    wba = sb("wba_s", [128, DEPTH, 8, 8], BF16)
    DMA("sp", gpre[:], gpre_d, w=["gpre"])
    DMA("sp", convw[:], convw_d, w=["convw"])
    DMA("sp", convsc[:], convsc_d, w=["convsc"])
    DMA("sp", negA[:].rearrange("p l h -> p (l h)"), alog_d.rearrange("l h -> (l h)").partition_broadcast(128), w=["negA"])
    DMA("sp", dtb[:].rearrange("p l h -> p (l h)"), dtb_d.rearrange("l h -> (l h)").partition_broadcast(128), w=["dtb"])
    DMA("sp", gnw[:].rearrange("p l f -> p (l f)"), gnw_d.rearrange("l f -> (l f)").partition_broadcast(128), w=["gnw"])
    E("act", "activation", negA[:], negA[:], AF.Exp, r=["negA"], w=["negA"])
    E("dve", "tensor_scalar", negA[:], negA[:], -1.0, None, ALU.mult, r=["negA"], w=["negA"])

    def cast_slab(dst, src2d, r0, nk, c0, wdt, key):
        d = dst.rearrange("p (k c) -> p k c", k=8)[:, 0:nk, 0:wdt]
        s = src2d[r0:r0 + nk * 128, c0:c0 + wdt].rearrange("(k p) c -> p k c", p=128)
        DMA("pool", d, s, w=[key])

    for l in range(DEPTH):
        for i in range(7):
            cast_slab(sc_in[l, i], w_in[l], 0, 8, i * 512, 512, ("sc", l, "in", i))
        for i in range(2):
            cast_slab(sc_o[l, i], w_o[l], 0, 8, i * 512, 512, ("sc", l, "o", i))
        for i in range(6):
            wdt = 512 if i < 5 else 256
            cast_slab(sc_g[l, i], w_g[l], 0, 8, i * 512, wdt, ("sc", l, "g", i))
            cast_slab(sc_u[l, i], w_u[l], 0, 8, i * 512, wdt, ("sc", l, "u", i))
        for hf in range(2):
            for q in range(3):
                cast_slab(sc_d[l, hf * 3 + q], w_d[l], q * 1024, 8 if q < 2 else 6, hf * 512, 512,
                          ("sc", l, "d", hf * 3 + q))
        DMA("pool", sc_ba[l].rearrange("p (k c) -> p k c", k=8),
            w_ba[l].rearrange("(k p) c -> p k c", p=128), w=[("sc", l, "ba")])
        DMA("sp", wba[:, l], sc_ba[l].rearrange("p (k c) -> p k c", k=8), r=[("sc", l, "ba")], w=["wba"])

    scr = {"in": sc_in, "o": sc_o, "g": sc_g, "u": sc_u, "d": sc_d}
    marks = {}
    marks["prologue"] = len(P.ops)

    NSLOT = 4
    wslots = [sb("wslot%d" % i, [128, 8, 512], BF16) for i in range(NSLOT)]
    LS = layer_slabs()
    SL_P1 = [x for x in LS if x[0] == "in" and x[1] < 3]
    SL_HALO2 = [x for x in LS if x[0] == "in" and x[1] in (4, 5)]
    plan = []
    for l in range(DEPTH):
        plan.append((l, SL_P1))
        for t in range(ntp):
            plan.append((l, SL_P1))
        plan.append((l, SL_HALO2))
        for t in range(ntp):
            plan.append((l, LS))
    if with_sample:
        for l in range(DEPTH):
            plan.append((l, LS))
    seq = []
    for l, sl in plan:
        for sdesc in sl:
            seq.append((l,) + sdesc)
    wstate = {"issued": 0}

    def w_issue_upto(n):
        while wstate["issued"] <= min(n, len(seq) - 1):
            j = wstate["issued"]
            l, kind, idx, nk, wdt = seq[j]
            slot = wslots[j % NSLOT]
            src = scr[kind][l, idx].rearrange("p (k c) -> p k c", k=8)[:, 0:nk, 0:wdt]
            DMA("sp", slot[:, 0:nk, 0:wdt], src, r=[("sc", l, kind, idx)], w=[("ws", j % NSLOT)])
            wstate["issued"] += 1

    wcur = {"n": 0}

    def w_next(kind, hold=0):
        n = wcur["n"]
        assert seq[n][1] == kind, (seq[n], kind)
        w_issue_upto(n + NSLOT - 1 - hold)
        wcur["n"] += 1
        return wslots[n % NSLOT], ("ws", n % NSLOT)

    psb = [st.enter_context(nc.psum_tensor("ps%d" % i, [128, 512], F32)) for i in range(8)]

    def pk(i):
        return ("ps", i)

    raws = {}

    def raw(name, words):
        raws[name] = sb("raw_" + name, [128, words])
        return raws[name]

    GD_W = 7936 + 1536 + (2304 if CFG['state'] == F32 else 0)
    for nm, wd in (("xtok", 4096), ("hs", 2048), ("hT", 2048), ("A", 12324), ("scpre", 2056), ("zs", 2048),
                   ("mixT", 2048), ("gpw", 1024), ("halo", 144), ("halosc", 32), ("S", 2048), ("Sb", 2048 if CFG["state"] == F32 else 1024),
                   ("sm", 768), ("smr", 16), ("junk", 512), ("gd", GD_W), ("dummy", 4), ("haloI", 72), ("haloscI", 16),
                   ("tmpCH", 12), ("maskt", 16), ("TTall", 1024)):
        raw(nm, wd)

    def view(rname, dt, parts, shape, off=0):
        flat = raws[rname][:]
        if dt == BF16:
            flat = flat.bitcast(BF16)
        n = int(np.prod(shape))
        v = flat[0:parts, off:off + n]
        if len(shape) == 1:
            return v
        names = "abcde"[:len(shape)]
        pat = "p (%s) -> p %s" % (" ".join(names), " ".join(names))
        kw = {names[i]: shape[i] for i in range(len(shape) - 1)}
        return v.rearrange(pat, **kw)

    def make_group(name, PB, TB, NS, TS, NT):
        T = PB * TB
        g = dict(name=name, PB=PB, TB=TB, NS=NS, TS=TS, NT=NT, T=T)
        g["xtok"] = view("xtok", F32, PB, [TB, DM])
        g["hs"] = view("hs", BF16, PB, [TB, DM])
        g["sq"] = view("hs", BF16, 128, [8, T])
        g["hT"] = view("hT", BF16, 128, [8, T])
        g["pre"] = view("A", F32, 128, [12, NS, TS + 3])
        g["post"] = view("A", F32, 128, [12, T], off=6180)
        g["ftmp"] = view("A", F32, PB, [TB, DM])
        g["actT"] = view("A", BF16, 128, [22, T], off=2 * 6180)
        g["scpre"] = view("scpre", F32, 128, [4, NS, TS + 2])
        g["zs"] = view("zs", F32, PB, [TB, 512])
        g["mixT"] = view("mixT", BF16, 128, [8, T])
        g["gpw"] = view("gpw", F32, PB, [DM])
        g["halo"] = view("halo", F32, 128, [DEPTH, 12, NS, 3])
        g["halosc"] = view("halosc", F32, 128, [DEPTH, 4, NS, 2])
        g["S"] = view("S", F32, 128, [DEPTH, NS, 4, 128])
        g["Sb"] = view("Sb", CFG["state"], 128, [DEPTH, NS, 4, 128])
        g["sm"] = view("sm", F32, PB, [24, TB, 8])
        g["smr"] = view("smr", F32, 128, [TB, 4])
        g["junk"] = view("junk", F32, 128, [512])
        g["TTall"] = view("TTall", BF16, PB, [TB, 4, PB])
        g["Sx"] = view("zs", F32, 128, [2, 4, 128])
        g["Sxb"] = view("zs", BF16, 128, [2, 4, 128], off=2048)
        g["vne"] = view("zs", BF16, PB, [2, 4, 128], off=3072)
        C = PB
        off = [0]

        def gv(dt, parts, shape):
            n = int(np.prod(shape))
            words = n if dt == F32 else (n + 1) // 2
            o = off[0]
            off[0] += words
            return view("gd", dt, parts, shape, off=o if dt == F32 else 2 * o)

        g["knT"] = gv(BF16, 128, [4, C])
        g["qnT"] = gv(BF16, 128, [4, C])
        g["qgT"] = gv(CFG["state"], 128, [4, C])
        g["diag"] = gv(F32, C, [1, 4, C])
        g["egrow"] = gv(F32, 128, [4, C])
        g["dtmp"] = gv(F32, C, [2, 4, C])
        g["Dm"] = gv(F32, C, [2, 4, C])
        g["Pk"] = [gv(CFG["chain"], C, [4, C]) for i in range(2)]
        g["Qk"] = [gv(CFG["chain"], C, [4, C]) for i in range(2)]
        g["TT"] = [gv(CFG["chain"], C, [4, C]) for i in range(2)]
        g["TTs"] = gv(CFG["state"], C, [4, C])
        g["N32"] = gv(F32, C, [4, C])
        g["TT32"] = gv(F32, C, [4, C])
        g["Tnat"] = gv(F32, C, [4, C])
        g["qkT"] = gv(CFG["state"], C, [4, C])
        g["kbeg"] = gv(CFG["state"], C, [4, 128])
        g["ktail"] = gv(CFG["state"], C, [4, 128])
        g["vb"] = gv(CFG["state"], C, [4, 128])
        g["nwT"] = gv(CFG["state"], 128, [4, C])
        g["vnew"] = gv(CFG["state"], C, [4, 128])
        g["og"] = gv(F32, C, [4, 128])
        g["ogb"] = gv(BF16, C, [4, 128])
        assert off[0] <= GD_W, off[0]
        if PB == 128 and CFG["chain"] == BF16:
            bw = lambda w_: 2 * w_
            g["cs1"] = dict(Pk=[view("mixT", BF16, C, [4, C], off=bw(0)), view("mixT", BF16, C, [4, C], off=bw(256))],
                            Qk=[view("mixT", BF16, C, [4, C], off=bw(512)), view("mixT", BF16, C, [4, C], off=bw(768))],
                            TT=[view("mixT", BF16, C, [4, C], off=bw(1024)), view("mixT", BF16, C, [4, C], off=bw(1280))],
                            N32=view("mixT", F32, C, [4, C], off=1536),
                            TT32=view("gpw", F32, C, [4, C], off=0), Tnat=view("gpw", F32, C, [4, C], off=512),
                            Tnat2=view("scpre", F32, C, [4, C], off=0), EvK="scpre",
                            TTs=view("TTall", BF16, C, [4, C], off=0), sfx="_1", bP=0, bQ=1, bT=3)
        return g

    def K(g, nm):
        return nm

    AKEYS = ["ftmp", "actT"] + ["post%d" % c for c in range(12)] + ["pre%d" % c for c in range(12)]

    def barrierA():
        d = raws["dummy"]
        E("act", "copy", d[:, 2:3], d[:, 0:1], r=[], w=AKEYS + ["dummy"])

    E("pool", "memset", raws["dummy"][:], 0.0, w=["dummy"])

    def norm_T(g, x, xkey, l, which):
        PB, TB, T = g["PB"], g["TB"], g["T"]
        sm = g["sm"]
        ss = sm[:, 0, :, 0:1]
        rstd = sm[:, 1, :, 0:1]
        E("dve", "memset", sm[:, 0, :, :], 0.0, w=[K(g, "sm0")])
        for b in range(TB):
            E("act", "activation", g["hs"][:, b, :], x[:, b, :], AF.Square, accum_out=sm[:, 0, b, 0:1],
              r=[xkey + str(b), K(g, "sm0")], w=[K(g, "hs"), K(g, "sm0")])
        E("act", "activation", rstd, ss, AF.Ln, bias=EPS, scale=1.0 / DM, r=[K(g, "sm0")], w=[K(g, "sm1")])
        E("act", "activation", rstd, rstd, AF.Exp, scale=-0.5, r=[K(g, "sm1")], w=[K(g, "sm1")])
        for b in range(TB):
            E("dve", "tensor_scalar", g["hs"][:, b, :], x[:, b, :], sm[:, 1, b, 0:1], None, ALU.mult,
              r=[xkey + str(b), K(g, "sm1")], w=[K(g, "hs")])
        for kc in range(8):
            bank = kc % 2
            pv = psb[bank][:].bitcast(BF16)
            for b in range(TB):
                E("pe", "transpose", pv[:, b * PB:(b + 1) * PB], g["hs"][:, b, kc * 128:(kc + 1) * 128],
                  identb[0:PB, 0:PB], r=[K(g, "hs"), "identb"], w=[pk(bank)])
            E("dve", "tensor_scalar", g["hT"][:, kc, :], pv[:, 0:T], gpre[:, l, which, kc:kc + 1], None, ALU.mult,
              r=[pk(bank), "gpre"], w=[K(g, "hT")])

    def tile_layer(g, x, xkey, l, first, last, out_conv, out_sc, out_state, mode="full", tt_store=None, tt_load=None, after_norm=None):
        full = (mode == "full")
        need_o = full
        need_chain = tt_load is None
        if tt_load is not None:
            for b_ in range(g["TB"]):
                DMA("pool", g["TTall"][:, b_].rearrange("p h c -> p (h c)"), tt_load(b_), r=[("tts", b_)], w=["TTall"])
        PB, TB, NS, TS, T = g["PB"], g["TB"], g["NS"], g["TS"], g["T"]
        C = PB
        sm = g["sm"]
        hT, pre, post = g["hT"], g["pre"], g["post"]
        barrierA()
        norm_T(g, x, xkey, l, 0)
        if after_norm is not None:
            after_norm()

        PREK = ["pre%d" % c for c in range(12)]
        E("pool", "tensor_copy", pre[:, :, :, 0:3], g["halo"][:, l], r=[K(g, "halo%d" % l)], w=PREK)

        def conv_chunk(c):
            pv = post[:, c, :].rearrange("p (s t) -> p s t", s=NS)
            E("dve", "tensor_scalar", pv, pre[:, c, :, 0:TS], convw[:, l, c, 0:1], None, ALU.mult,
              r=["pre%d" % c, "convw"], w=[K(g, "post%d" % c)])
            for i in (1, 2, 3):
                E("dve", "scalar_tensor_tensor", pv, pre[:, c, :, i:i + TS], convw[:, l, c, i:i + 1], pv,
                  ALU.mult, ALU.add, r=["pre%d" % c, K(g, "post%d" % c), "convw"], w=[K(g, "post%d" % c)])
            E("act", "activation", post[:, c, :], post[:, c, :], AF.Silu, r=[K(g, "post%d" % c)], w=[K(g, "post%d" % c)])
            if c < 8:
                E("act", "activation", g["sq"][:, c, :], post[:, c, :], AF.Square,
                  r=[K(g, "post%d" % c)], w=[K(g, "hs")])

        early_conv = (lambda s_: True) if not full else (lambda s_: s_ < 2)
        if full:
            E("pool", "tensor_copy", g["scpre"][:, :, :, 0:2], g["halosc"][:, l], r=[K(g, "halosc%d" % l)], w=[K(g, "scpre")])

        marks.setdefault("norm", len(P.ops))
        bank_rr = [0]

        def nb():
            b = bank_rr[0] % 4
            bank_rr[0] += 1
            return b

        for s in range(3):
            slot, skey = w_next("in")
            for oc in range(4):
                c = s * 4 + oc
                bk = nb()
                for kc in range(8):
                    E("pe", "matmul", psb[bk][:, 0:T], lhsT=slot[:, kc, oc * 128:(oc + 1) * 128], rhs=hT[:, kc, :],
                      start=(kc == 0), stop=(kc == 7), r=[skey, K(g, "hT")], w=[pk(bk)])
                E("act", "copy", pre[:, c, :, 3:3 + TS], psb[bk][:, 0:T].rearrange("p (s t) -> p s t", s=NS),
                  r=[pk(bk)], w=["pre%d" % c])
            if s > 0 and early_conv(s - 1):
                for oc in range(4):
                    conv_chunk((s - 1) * 4 + oc)
        if early_conv(2):
            for oc in range(4):
                conv_chunk(8 + oc)
        if full:
            slot, skey = w_next("in")
            for b in range(TB):
                bk = nb()
                for kc in range(8):
                    E("pe", "matmul", psb[bk][0:PB, :], lhsT=hT[:, kc, b * PB:(b + 1) * PB], rhs=slot[:, kc, :],
                      start=(kc == 0), stop=(kc == 7), r=[skey, K(g, "hT")], w=[pk(bk)])
                E("act", "activation", g["zs"][:, b, :], psb[bk][0:PB, :], AF.Silu, r=[pk(bk)], w=[K(g, "zs")])
        bk = nb()
        for b in range(TB):
            for kc in range(8):
                E("pe", "matmul", psb[bk][0:PB, b * 8:(b + 1) * 8], lhsT=hT[:, kc, b * PB:(b + 1) * PB],
                  rhs=wba[:, l, kc, :], start=(kc == 0), stop=(kc == 7), r=["wba", K(g, "hT")], w=[pk(bk)])
        E("dve", "tensor_copy", sm[:, 2, :, :], psb[bk][0:PB, 0:TB * 8].rearrange("p (b c) -> p b c", b=TB),
          r=[pk(bk)], w=[K(g, "sm2")])
        if full:
            slot, skey = w_next("in")
            for oc in range(4):
                bk = nb()
                for kc in range(8):
                    E("pe", "matmul", psb[bk][:, 0:T], lhsT=slot[:, kc, oc * 128:(oc + 1) * 128], rhs=hT[:, kc, :],
                      start=(kc == 0), stop=(kc == 7), r=[skey, K(g, "hT")], w=[pk(bk)])
                E("act", "copy", g["scpre"][:, oc, :, 2:2 + TS], psb[bk][:, 0:T].rearrange("p (s t) -> p s t", s=NS),
                  r=[pk(bk)], w=[K(g, "scpre")])
            slot, skey = w_next("in")
            for oc in range(4):
                bk = nb()
                for kc in range(8):
                    E("pe", "matmul", psb[bk][:, 0:T], lhsT=slot[:, kc, oc * 128:(oc + 1) * 128], rhs=hT[:, kc, :],
                      start=(kc == 0), stop=(kc == 7), r=[skey, K(g, "hT")], w=[pk(bk)])
                E("dve", "tensor_tensor", g["scpre"][:, oc, :, 2:2 + TS], g["scpre"][:, oc, :, 2:2 + TS],
                  psb[bk][:, 0:T].rearrange("p (s t) -> p s t", s=NS), ALU.mult,
                  r=[pk(bk), K(g, "scpre")], w=[K(g, "scpre")])
            for oc in range(4):
                yv = post[:, 8 + oc, :].rearrange("p (s t) -> p s t", s=NS)
                E("dve", "tensor_scalar", yv, g["scpre"][:, oc, :, 0:TS], convsc[:, l, oc, 0:1], None, ALU.mult,
                  r=[K(g, "scpre"), "convsc"], w=[K(g, "post%d" % (8 + oc))])
                for i in (1, 2):
                    E("dve", "scalar_tensor_tensor", yv, g["scpre"][:, oc, :, i:i + TS], convsc[:, l, oc, i:i + 1], yv,
                      ALU.mult, ALU.add, r=[K(g, "scpre"), K(g, "post%d" % (8 + oc)), "convsc"], w=[K(g, "post%d" % (8 + oc))])
            E("pool", "tensor_copy", g["halosc"][:, l], g["scpre"][:, :, :, TS:TS + 2], r=[K(g, "scpre")], w=[K(g, "halosc%d" % l)])
            if last:
                for s in range(NS):
                    for r2 in range(2):
                        DMA("pool", out_sc(l, s)[r2:r2 + 1, :].rearrange("r (c p) -> p c r", p=128),
                            g["halosc"][:, l, :, s, r2:r2 + 1], r=[K(g, "halosc%d" % l)], allow_slow_non_contiguous=True)
            slot, skey = w_next("in")
            for oc in range(4):
                bk = nb()
                for kc in range(8):
                    E("pe", "matmul", psb[bk][:, 0:T], lhsT=slot[:, kc, oc * 128:(oc + 1) * 128], rhs=hT[:, kc, :],
                      start=(kc == 0), stop=(kc == 7), r=[skey, K(g, "hT")], w=[pk(bk)])
                E("dve", "tensor_tensor", g["mixT"][:, 4 + oc, :], post[:, 8 + oc, :], psb[bk][:, 0:T], ALU.mult,
                  r=[pk(bk), K(g, "post%d" % (8 + oc))], w=[K(g, "mixT")])

        marks.setdefault("win", len(P.ops))
        for c in range(12):
            if not early_conv(c // 4):
                conv_chunk(c)
        E("pool", "tensor_copy", g["halo"][:, l], pre[:, :, :, TS:TS + 3], r=PREK, w=[K(g, "halo%d" % l)])
        if last and full:
            for s in range(NS):
                for r3 in range(3):
                    DMA("pool", out_conv(l, s)[r3:r3 + 1, :].rearrange("r (c p) -> p c r", p=128),
                        g["halo"][:, l, :, s, r3:r3 + 1], r=[K(g, "halo%d" % l)], allow_slow_non_contiguous=True)

        marks.setdefault("conv", len(P.ops))
        bk = nb()
        for b in range(TB):
            for c in range(8):
                E("pe", "matmul", psb[bk][0:PB, b * 8 + c:b * 8 + c + 1], lhsT=g["sq"][:, c, b * PB:(b + 1) * PB],
                  rhs=onesb[:, 0:1], start=True, stop=True, r=[K(g, "hs"), "onesb"], w=[pk(bk)])
        E("act", "activation", sm[:, 3, :, :], psb[bk][0:PB, 0:TB * 8].rearrange("p (b c) -> p b c", b=TB), AF.Ln,
          bias=EPS, scale=1.0, r=[pk(bk)], w=[K(g, "sm3")])
        E("act", "activation", sm[:, 3, :, :], sm[:, 3, :, :], AF.Exp, scale=-0.5, r=[K(g, "sm3")], w=[K(g, "sm3")])
        rq = sm[:, 3, :, 0:4]
        rk = sm[:, 3, :, 4:8]
        beta = sm[:, 4, :, 0:4]
        E("act", "activation", beta, sm[:, 2, :, 0:4], AF.Exp, scale=-1.0, r=[K(g, "sm2")], w=[K(g, "sm4")])
        E("dve", "tensor_scalar", beta, beta, 1.0, None, ALU.add, r=[K(g, "sm4")], w=[K(g, "sm4")])
        E("dve", "reciprocal", beta, beta, r=[K(g, "sm4")], w=[K(g, "sm4")])
        glog = sm[:, 4, :, 4:8]
        E("dve", "tensor_tensor", glog, sm[:, 2, :, 4:8], dtb[0:PB, l, :].unsqueeze(1).to_broadcast([PB, TB, 4]), ALU.add,
          r=[K(g, "sm2"), "dtb"], w=[K(g, "sm4")])
        E("act", "activation", glog, glog, AF.Exp, r=[K(g, "sm4")], w=[K(g, "sm4")])
        E("act", "activation", glog, glog, AF.Ln, bias=1.0, scale=1.0, r=[K(g, "sm4")], w=[K(g, "sm4")])
        E("dve", "tensor_tensor", glog, glog, negA[0:PB, l, :].unsqueeze(1).to_broadcast([PB, TB, 4]), ALU.mult,
          r=[K(g, "sm4"), "negA"], w=[K(g, "sm4")])
        bk = nb()
        for b in range(TB):
            E("pe", "matmul", psb[bk][0:PB, b * 8:b * 8 + 4], lhsT=Umat[0:PB, 0:PB], rhs=sm[:, 4, b, 4:8],
              start=True, stop=True, r=[K(g, "sm4"), "Umat"], w=[pk(bk)])
            E("pe", "matmul", psb[bk][0:PB, b * 8 + 4:b * 8 + 8], lhsT=ones[0:PB, 0:PB], rhs=sm[:, 4, b, 4:8],
              start=True, stop=True, r=[K(g, "sm4"), "ones"], w=[pk(bk)])
            E("pe", "matmul", psb[bk][:, 64 + b * 4:64 + b * 4 + 4], lhsT=ones[0:PB, :], rhs=sm[:, 4, b, 4:8],
              start=True, stop=True, r=[K(g, "sm4"), "ones"], w=[pk(bk)])
        E("dve", "tensor_copy", sm[:, 5, :, :], psb[bk][0:PB, 0:TB * 8].rearrange("p (b c) -> p b c", b=TB),
          r=[pk(bk)], w=[K(g, "sm5")])
        E("act", "activation", g["smr"][:], psb[bk][:, 64:64 + TB * 4].rearrange("p (b c) -> p b c", b=TB), AF.Exp,
          r=[pk(bk)], w=[K(g, "smr")])
        gcum = sm[:, 5, :, 0:4]
        glast = sm[:, 5, :, 4:8]
        eg = sm[:, 6, :, 0:4]
        etail = sm[:, 6, :, 4:8]
        E("act", "activation", eg, gcum, AF.Exp, r=[K(g, "sm5")], w=[K(g, "sm6")])
        E("dve", "tensor_tensor", etail, glast, gcum, ALU.subtract, r=[K(g, "sm5")], w=[K(g, "sm6")])
        E("act", "activation", etail, etail, AF.Exp, r=[K(g, "sm6")], w=[K(g, "sm6")])
        ckbeg = sm[:, 7, :, 0:4]
        cktail = sm[:, 7, :, 4:8]
        E("dve", "tensor_tensor", ckbeg, rk, beta, ALU.mult, r=[K(g, "sm3"), K(g, "sm4")], w=[K(g, "sm7")])
        E("dve", "tensor_tensor", ckbeg, ckbeg, eg, ALU.mult, r=[K(g, "sm7"), K(g, "sm6")], w=[K(g, "sm7")])
        E("dve", "tensor_tensor", cktail, rk, etail, ALU.mult, r=[K(g, "sm3"), K(g, "sm6")], w=[K(g, "sm7")])
        rqs = sm[:, 8, :, 0:4]
        nbeta = sm[:, 8, :, 4:8]
        E("dve", "tensor_scalar", rqs, rq, 128.0 ** -0.5, None, ALU.mult, r=[K(g, "sm3")], w=[K(g, "sm8")])
        E("dve", "tensor_scalar", nbeta, beta, -1.0, None, ALU.mult, r=[K(g, "sm4")], w=[K(g, "sm8")])

        marks.setdefault("scal", len(P.ops))
        S = g["S"]
        Sb = g["Sb"]
        def block(b, cs):
            CK = lambda nm: nm + cs["sfx"]
            slot_s = b if NS > 1 else 0
            skS = K(g, "S%d_%d" % (l, slot_s))
            tok = slice(b * PB, (b + 1) * PB)
            R = []
            for vi, (vec, vk) in enumerate(((rk, "sm3"), (rqs, "sm8"), (gcum, "sm5"))):
                if vi == 1 and not need_o:
                    R.append(None)
                    continue
                E("dve", "tensor_tensor", g["diag"][:, 0], ident[0:C, 0:C].unsqueeze(1).to_broadcast([C, 4, C]),
                  vec[:, b, :].unsqueeze(2).to_broadcast([C, 4, C]), ALU.mult, r=["ident", K(g, vk)], w=[K(g, "diag")])
                bk = 4 + vi
                E("pe", "matmul", psb[bk][:, 0:4 * C], lhsT=ones[0:C, :], rhs=g["diag"][:, 0].rearrange("p h c -> p (h c)"),
                  start=True, stop=True, r=["ones", K(g, "diag")], w=[pk(bk)])
                R.append(psb[bk][:, 0:4 * C].rearrange("p (h c) -> p h c", h=4))
            marks.setdefault("r1", len(P.ops))
            kraw = post[:, 4:8, tok]
            qraw = post[:, 0:4, tok]
            E("dve", "tensor_tensor", g["knT"][:], kraw, R[0], ALU.mult, r=[K(g, "post%d" % c) for c in range(4, 8)] + [pk(4)],
              w=[K(g, "knT")])
            if need_o:
                E("dve", "tensor_tensor", g["qnT"][:], qraw, R[1], ALU.mult, r=[K(g, "post%d" % c) for c in range(0, 4)] + [pk(5)],
                  w=[K(g, "qnT")])
                E("act", "activation", g["egrow"][:], R[2], AF.Exp, r=[pk(6)], w=[K(g, "egrow")])
            marks.setdefault("r2", len(P.ops))
            Rg = g["diag"][:, 0]
            E("act", "copy", Rg, psb[6][0:C, 0:4 * C].rearrange("p (h c) -> p h c", h=4), r=[pk(6)], w=[K(g, "diag")])
            gc_b = gcum[:, b, :].unsqueeze(2).to_broadcast([C, 4, C])
            if need_chain:
                E("dve", "tensor_tensor", g["dtmp"][:, 0], Rg, gc_b, ALU.subtract, r=[K(g, "diag"), K(g, "sm5")], w=[K(g, "dtmp0")])
                E("dve", "tensor_tensor", g["dtmp"][:, 0], g["dtmp"][:, 0], mstrict[0:C, 0:C].unsqueeze(1).to_broadcast([C, 4, C]),
                  ALU.add, r=[K(g, "dtmp0"), "mstrict"], w=[K(g, "dtmp0")])
                E("act", "activation", g["Dm"][:, 0], g["dtmp"][:, 0], AF.Exp, scale=-1.0, r=[K(g, "dtmp0")], w=[K(g, "Dm0")])
            if need_o:
                E("dve", "tensor_tensor", g["dtmp"][:, 1], gc_b, Rg, ALU.subtract, r=[K(g, "diag"), K(g, "sm5")], w=[K(g, "dtmp1")])
                E("dve", "tensor_tensor", g["dtmp"][:, 1], g["dtmp"][:, 1], minclT[0:C, 0:C].unsqueeze(1).to_broadcast([C, 4, C]),
                  ALU.add, r=[K(g, "dtmp1"), "minclT"], w=[K(g, "dtmp1")])
                E("act", "activation", g["Dm"][:, 1], g["dtmp"][:, 1], AF.Exp, scale=-1.0, r=[K(g, "dtmp1")], w=[K(g, "Dm1")])
                E("dve", "tensor_tensor", g["qgT"][:], g["qnT"][:], g["egrow"][:], ALU.mult, r=[K(g, "qnT"), K(g, "egrow")],
                  w=[K(g, "qgT")])
            marks.setdefault("g1", len(P.ops))
            for h in range(4):
                if need_chain:
                    E("pe", "matmul", psb[4][0:C, h * C:(h + 1) * C], lhsT=g["knT"][:, h, :], rhs=g["knT"][:, h, :],
                      start=True, stop=True, r=[K(g, "knT")], w=[pk(4)])
            for h in range(4):
                if need_o:
                    E("pe", "matmul", psb[5][0:C, h * C:(h + 1) * C], lhsT=g["knT"][:, h, :], rhs=g["qnT"][:, h, :],
                      start=True, stop=True, r=[K(g, "knT"), K(g, "qnT")], w=[pk(5)])
            KKv = psb[4][0:C, 0:4 * C].rearrange("p (h c) -> p h c", h=4)
            QKv = psb[5][0:C, 0:4 * C].rearrange("p (h c) -> p h c", h=4)
            if need_o:
                E("dve", "tensor_tensor", g["qkT"][:], QKv, g["Dm"][:, 1], ALU.mult, r=[pk(5), K(g, "Dm1")], w=[K(g, "qkT")])
            if need_chain:
                N32 = cs["N32"]
                E("dve", "tensor_tensor", N32, KKv, g["Dm"][:, 0], ALU.mult,
                  r=[pk(4), K(g, "Dm0")], w=[CK("N32")])
                E("dve", "tensor_tensor", N32, N32, nbeta[:, b, :].unsqueeze(2).to_broadcast([C, 4, C]), ALU.mult,
                  r=[CK("N32"), K(g, "sm8")], w=[CK("N32")])
                E("act", "copy", cs["Pk"][0][:], N32, r=[CK("N32")], w=[CK("Pk0")])
                for h in range(4):
                    E("pe", "transpose", psb[6][0:C, h * C:(h + 1) * C], N32[:, h, :], ident[0:C, 0:C],
                      r=[CK("N32"), "ident"], w=[pk(6)])
                E("act", "copy", cs["Qk"][0][:], psb[6][0:C, 0:4 * C].rearrange("p (h c) -> p h c", h=4), r=[pk(6)], w=[CK("Qk0")])
                marks.setdefault("g2", len(P.ops))
                yield "front"
                nlev = int(np.log2(C))
                idc = identb if CFG["chain"] == BF16 else ident
                TTp = psb[cs["bT"]][0:C, 0:4 * C].rearrange("p (h c) -> p h c", h=4)
                for h in range(4):
                    E("pe", "matmul", psb[cs["bT"]][0:C, h * C:(h + 1) * C], lhsT=idc[0:C, 0:C], rhs=idc[0:C, 0:C],
                      start=True, stop=False, r=["identb", "ident"], w=[pk(cs["bT"])])
                    E("pe", "matmul", psb[cs["bT"]][0:C, h * C:(h + 1) * C], lhsT=idc[0:C, 0:C], rhs=cs["Qk"][0][:, h, :],
                      start=False, stop=True, r=["identb", "ident", CK("Qk0")], w=[pk(cs["bT"])])
                cur = 0
                for lev in range(1, nlev):
                    nxt = 1 - cur
                    E("dve", "tensor_copy", cs["TT"][cur][:], TTp, r=[pk(cs["bT"])], w=[CK("TT%d" % cur)])
                    for h in range(4):
                        E("pe", "matmul", psb[cs["bP"]][0:C, h * C:(h + 1) * C], lhsT=cs["Qk"][cur][:, h, :], rhs=cs["Pk"][cur][:, h, :],
                          start=True, stop=True, r=[CK("Qk%d" % cur), CK("Pk%d" % cur)], w=[pk(cs["bP"])])
                    E("act", "copy", cs["Pk"][nxt][:], psb[cs["bP"]][0:C, 0:4 * C].rearrange("p (h c) -> p h c", h=4),
                      r=[pk(cs["bP"])], w=[CK("Pk%d" % nxt)])
                    if lev < nlev - 1:
                        for h in range(4):
                            E("pe", "matmul", psb[cs["bQ"]][0:C, h * C:(h + 1) * C], lhsT=cs["Pk"][cur][:, h, :], rhs=cs["Qk"][cur][:, h, :],
                              start=True, stop=True, r=[CK("Qk%d" % cur), CK("Pk%d" % cur)], w=[pk(cs["bQ"])])
                        E("dve", "tensor_copy", cs["Qk"][nxt][:], psb[cs["bQ"]][0:C, 0:4 * C].rearrange("p (h c) -> p h c", h=4),
                          r=[pk(cs["bQ"])], w=[CK("Qk%d" % nxt)])
                    for h in range(4):
                        E("pe", "matmul", psb[cs["bT"]][0:C, h * C:(h + 1) * C], lhsT=idc[0:C, 0:C], rhs=cs["TT"][cur][:, h, :],
                          start=True, stop=False, r=["identb", "ident", CK("TT%d" % cur)], w=[pk(cs["bT"])])
                        E("pe", "matmul", psb[cs["bT"]][0:C, h * C:(h + 1) * C], lhsT=cs["Pk"][nxt][:, h, :], rhs=cs["TT"][cur][:, h, :],
                          start=False, stop=True, r=[CK("Pk%d" % nxt), CK("TT%d" % cur)], w=[pk(cs["bT"])])
                    cur = nxt
                    yield "lev"
                if CFG["chain"] == F32:
                    E("dve", "tensor_copy", cs["TTs"][:], TTp, r=[pk(cs["bT"])], w=[CK("TTs")])
                else:
                    TT32, Tn = cs["TT32"], cs["Tnat"]
                    bP, bQ = cs["bP"], cs["bQ"]
                    E("act", "copy", TT32, TTp, r=[pk(cs["bT"])], w=[CK("TT32")])
                    for h in range(4):
                        E("pe", "matmul", psb[bP][0:C, h * C:(h + 1) * C], lhsT=N32[:, h, :], rhs=TT32[:, h, :],
                          start=True, stop=True, r=[CK("N32"), CK("TT32")], w=[pk(bP)])
                    for h in range(4):
                        E("pe", "transpose", psb[bQ][0:C, h * C:(h + 1) * C], TT32[:, h, :], ident[0:C, 0:C],
                          r=[CK("TT32"), "ident"], w=[pk(bQ)])
                    Ev = cs["Tnat2"]
                    E("dve", "tensor_tensor", Ev, psb[bP][0:C, 0:4 * C].rearrange("p (h c) -> p h c", h=4), TT32, ALU.subtract,
                      r=[pk(bP), CK("TT32")], w=[cs["EvK"]])
                    E("dve", "tensor_tensor", Ev, Ev, ident[0:C, 0:C].unsqueeze(1).to_broadcast([C, 4, C]), ALU.add,
                      r=[cs["EvK"], "ident"], w=[cs["EvK"]])
                    E("act", "copy", Tn, psb[bQ][0:C, 0:4 * C].rearrange("p (h c) -> p h c", h=4), r=[pk(bQ)], w=[CK("Tnat")])
                    for h in range(4):
                        E("pe", "matmul", psb[bP][0:C, h * C:(h + 1) * C], lhsT=Tn[:, h, :], rhs=Ev[:, h, :],
                          start=True, stop=True, r=[CK("Tnat"), cs["EvK"]], w=[pk(bP)])
                    E("dve", "tensor_tensor", cs["TTs"][:], psb[bP][0:C, 0:4 * C].rearrange("p (h c) -> p h c", h=4), TT32, ALU.add,
                      r=[pk(bP), CK("TT32")], w=[CK("TTs")])
                yield "chain"
            TTf = cs["TTs"]
            cur = "s"
            if tt_store is not None:
                DMA("pool", tt_store(b), TTf.rearrange("p h c -> p (h c)"), r=[CK("TTs")], w=[("tts", b)])
            if tt_load is not None:
                TTf = g["TTall"][:, b]
                cur = "all"
            marks.setdefault("g3", len(P.ops))
            for h in range(4):
                E("pe", "transpose", psb[4][0:C, h * 128:(h + 1) * 128], post[:, 4 + h, tok], ident[:],
                  r=[K(g, "post%d" % (4 + h)), "ident"], w=[pk(4)])
            for h in range(4):
                E("pe", "transpose", psb[5][0:C, h * 128:(h + 1) * 128], post[:, 8 + h, tok], ident[:],
                  r=[K(g, "post%d" % (8 + h)), "ident"], w=[pk(5)])
            ktok = psb[4][0:C, :].rearrange("p (h c) -> p h c", h=4)
            vtok = psb[5][0:C, :].rearrange("p (h c) -> p h c", h=4)
            for h in range(4):
                E("dve", "tensor_scalar", g["kbeg"][:, h, :], ktok[:, h, :], ckbeg[:, b, h:h + 1], None, ALU.mult,
                  r=[pk(4), K(g, "sm7")], w=[K(g, "kbeg")])
                E("dve", "tensor_scalar", g["ktail"][:, h, :], ktok[:, h, :], cktail[:, b, h:h + 1], None, ALU.mult,
                  r=[pk(4), K(g, "sm7")], w=[K(g, "ktail")])
                E("dve", "tensor_scalar", g["vb"][:, h, :], vtok[:, h, :], beta[:, b, h:h + 1], None, ALU.mult,
                  r=[pk(5), K(g, "sm4")], w=[K(g, "vb")])
            for h in range(4):
                E("pe", "matmul", psb[6][:, h * C:(h + 1) * C], lhsT=g["kbeg"][:, h, :], rhs=TTf[:, h, :],
                  start=True, stop=True, r=[K(g, "kbeg"), CK("TTs"), "TTall"], w=[pk(6)])
            E("dve", "tensor_scalar", g["nwT"][:], psb[6][:, 0:4 * C].rearrange("p (h c) -> p h c", h=4), -1.0, None, ALU.mult,
              r=[pk(6)], w=[K(g, "nwT")])
            marks.setdefault("g4", len(P.ops))
            if full:
                for h in range(4):
                    E("pe", "matmul", psb[4][0:C, h * 128:(h + 1) * 128], lhsT=TTf[:, h, :], rhs=g["vb"][:, h, :],
                      start=True, stop=False, r=[CK("TTs"), "TTall", K(g, "vb")], w=[pk(4)])
                    E("pe", "matmul", psb[4][0:C, h * 128:(h + 1) * 128], lhsT=g["nwT"][:, h, :], rhs=Sb[:, l, slot_s, h, :],
                      start=False, stop=True, r=[K(g, "nwT"), skS + "b"], w=[pk(4)])
                E("act", "copy", g["vnew"][:], psb[4][0:C, :].rearrange("p (h c) -> p h c", h=4), r=[pk(4)], w=[K(g, "vnew")])
                for h in range(4):
                    E("pe", "matmul", psb[5][0:C, h * 128:(h + 1) * 128], lhsT=g["qgT"][:, h, :], rhs=Sb[:, l, slot_s, h, :],
                      start=True, stop=False, r=[K(g, "qgT"), skS + "b"], w=[pk(5)])
                    E("pe", "matmul", psb[5][0:C, h * 128:(h + 1) * 128], lhsT=g["qkT"][:, h, :], rhs=g["vnew"][:, h, :],
                      start=False, stop=True, r=[K(g, "qkT"), K(g, "vnew")], w=[pk(5)])
                for h in range(4):
                    E("pe", "matmul", psb[6][:, h * 128:(h + 1) * 128], lhsT=g["ktail"][:, h, :], rhs=g["vnew"][:, h, :],
                      start=True, stop=True, r=[K(g, "ktail"), K(g, "vnew")], w=[pk(6)])
                Sv = S[:, l, slot_s]
                E("dve", "tensor_tensor", Sv, Sv, g["smr"][:, b, :].unsqueeze(2).to_broadcast([128, 4, 128]), ALU.mult,
                  r=[skS, K(g, "smr")], w=[skS])
                E("dve", "tensor_tensor", Sv, Sv, psb[6][:, :].rearrange("p (h c) -> p h c", h=4), ALU.add,
                  r=[skS, pk(6)], w=[skS])
                E("act", "copy", Sb[:, l, slot_s], Sv, r=[skS], w=[skS + "b"])
                marks.setdefault("g5", len(P.ops))
                ov = psb[5][0:C, :].rearrange("p (h c) -> p h c", h=4)
                E("dve", "memset", sm[:, 9, b, :], 0.0, w=[K(g, "sm9")])
                for h in range(4):
                    E("act", "activation", g["junk"][0:C, 0:128], psb[5][0:C, h * 128:(h + 1) * 128], AF.Square,
                      accum_out=sm[:, 9, b, h:h + 1], r=[pk(5), K(g, "sm9")], w=[K(g, "junk"), K(g, "sm9")])
                E("act", "activation", sm[:, 9, b, 0:4], sm[:, 9, b, 0:4], AF.Ln, bias=EPS, scale=1.0 / 128, r=[K(g, "sm9")], w=[K(g, "sm9")])
                E("act", "activation", sm[:, 9, b, 0:4], sm[:, 9, b, 0:4], AF.Exp, scale=-0.5, r=[K(g, "sm9")], w=[K(g, "sm9")])
                for h in range(4):
                    E("dve", "tensor_scalar", g["og"][:, h, :], ov[:, h, :], sm[:, 9, b, h:h + 1], None, ALU.mult,
                      r=[pk(5), K(g, "sm9")], w=[K(g, "og")])
                E("dve", "tensor_tensor", g["og"][:], g["og"][:], gnw[0:C, l, :].unsqueeze(1).to_broadcast([C, 4, 128]), ALU.mult,
                  r=[K(g, "og"), "gnw"], w=[K(g, "og")])
                E("dve", "tensor_tensor", g["ogb"][:], g["og"][:], g["zs"][:, b, :].rearrange("p (h c) -> p h c", h=4), ALU.mult,
                  r=[K(g, "og"), K(g, "zs")], w=[K(g, "ogb")])
                pvb = psb[7][:].bitcast(BF16)
                for h in range(4):
                    E("pe", "transpose", pvb[:, h * C:(h + 1) * C], g["ogb"][:, h, :], identb[0:C, 0:C],
                      r=[K(g, "ogb"), "identb"], w=[pk(7)])
                E("dve", "tensor_copy", g["mixT"][:, 0:4, tok], pvb[:, 0:4 * C].rearrange("p (h c) -> p h c", h=4),
                  r=[pk(7)], w=[K(g, "mixT")])
            else:
                Sx, Sxb, vne = g["Sx"], g["Sxb"], g["vne"]
                for h in range(4):
                    E("pe", "matmul", psb[4][0:C, h * 128:(h + 1) * 128], lhsT=TTf[:, h, :], rhs=g["vb"][:, h, :],
                      start=True, stop=False, r=[CK("TTs"), "TTall", K(g, "vb")], w=[pk(4)])
                    E("pe", "matmul", psb[4][0:C, h * 128:(h + 1) * 128], lhsT=g["nwT"][:, h, :], rhs=Sxb[:, 0, h, :],
                      start=False, stop=True, r=[K(g, "nwT"), "Sxb"], w=[pk(4)])
                for h in range(4):
                    E("pe", "matmul", psb[5][0:C, h * 128:(h + 1) * 128], lhsT=g["nwT"][:, h, :], rhs=Sxb[:, 1, h, :],
                      start=True, stop=True, r=[K(g, "nwT"), "Sxb"], w=[pk(5)])
                E("act", "copy", vne[:, 0], psb[4][0:C, :].rearrange("p (h c) -> p h c", h=4), r=[pk(4)], w=["vne"])
                E("dve", "tensor_copy", vne[:, 1], psb[5][0:C, :].rearrange("p (h c) -> p h c", h=4), r=[pk(5)], w=["vne"])
                for part in range(2):
                    bkx = 6 + part
                    for h in range(4):
                        E("pe", "matmul", psb[bkx][:, h * 128:(h + 1) * 128], lhsT=g["ktail"][:, h, :], rhs=vne[:, part, h, :],
                          start=True, stop=True, r=[K(g, "ktail"), "vne"], w=[pk(bkx)])
                    E("dve", "tensor_tensor", Sx[:, part], Sx[:, part], g["smr"][:, b, :].unsqueeze(2).to_broadcast([128, 4, 128]),
                      ALU.mult, r=["Sx", K(g, "smr")], w=["Sx"])
                    E("dve", "tensor_tensor", Sx[:, part], Sx[:, part], psb[bkx][:, :].rearrange("p (h c) -> p h c", h=4), ALU.add,
                      r=["Sx", pk(bkx)], w=["Sx"])
                E("act", "copy", Sxb[:], Sx[:], r=["Sx"], w=["Sxb"])
        cs0 = dict(Pk=g["Pk"], Qk=g["Qk"], TT=g["TT"], TTs=g["TTs"], N32=g["N32"], TT32=g["TT32"], Tnat=g["Tnat"],
                   Tnat2=g["dtmp"][:, 0], EvK="dtmp0", sfx="", bP=4, bQ=5, bT=7)
        if mode == "p1" and "cs1" in g and TB % 2 == 0 and need_chain:
            for b0 in range(0, TB, 2):
                ga, gb = block(b0, cs0), block(b0 + 1, g["cs1"])
                next(ga)
                next(gb)
                done = [False, False]
                while not all(done):
                    for i_, gg in enumerate((ga, gb)):
                        if not done[i_]:
                            if next(gg) == "chain":
                                done[i_] = True
                for _ in ga:
                    pass
                for _ in gb:
                    pass
        else:
            for b in range(TB):
                for _ in block(b, cs0):
                    pass
        if last and full:
            for s in range(NS):
                DMA("pool", out_state(l, s).rearrange("h p v -> p h v"), S[:, l, s], r=[K(g, "S%d_%d" % (l, s))])

        if full:
            marks.setdefault("gdn", len(P.ops))
            def post_norm_residual(which, mm_half):
                E("dve", "memset", sm[:, 10, :, :], 0.0, w=[K(g, "sm10")])
                DMA("act", g["gpw"], gpost_d[l, which].partition_broadcast(PB), w=["gpost"])
                for hf in range(2):
                    mm_half(hf)
                E("dve", "tensor_tensor", sm[:, 10, :, 0:1], sm[:, 10, :, 0:1], sm[:, 10, :, 1:2], ALU.add, r=[K(g, "sm10")], w=[K(g, "sm10")])
                E("act", "activation", sm[:, 10, :, 0:1], sm[:, 10, :, 0:1], AF.Ln, bias=EPS, scale=1.0 / DM, r=[K(g, "sm10")], w=[K(g, "sm10")])
                E("act", "activation", sm[:, 10, :, 0:1], sm[:, 10, :, 0:1], AF.Exp, scale=-0.5, r=[K(g, "sm10")], w=[K(g, "sm10")])
                for b in range(TB):
                    E("dve", "scalar_tensor_tensor", g["ftmp"][:, b, :], g["ftmp"][:, b, :], sm[:, 10, b, 0:1], g["gpw"],
                      ALU.mult, ALU.mult, r=[K(g, "ftmp"), K(g, "sm10"), "gpost"], w=[K(g, "ftmp")])
                    E("dve", "tensor_tensor", x[:, b, :], x[:, b, :], g["ftmp"][:, b, :], ALU.add, r=[K(g, "ftmp"), xkey + str(b)], w=[xkey + str(b)])

            def evac_half(bk, b, hf):
                E("act", "copy", g["ftmp"][:, b, hf * 512:(hf + 1) * 512], psb[bk][0:PB, :], r=[pk(bk)], w=[K(g, "ftmp")])
                E("act", "activation", g["junk"][0:PB, 0:512], psb[bk][0:PB, :], AF.Square, accum_out=sm[:, 10, b, hf:hf + 1],
                  r=[pk(bk), K(g, "sm10")], w=[K(g, "junk"), K(g, "sm10")])

            def wo_half(hf):
                slot, skey = w_next("o")
                for b in range(TB):
                    bk = nb()
                    for kc in range(8):
                        E("pe", "matmul", psb[bk][0:PB, :], lhsT=g["mixT"][:, kc, b * PB:(b + 1) * PB], rhs=slot[:, kc, :],
                          start=(kc == 0), stop=(kc == 7), r=[skey, K(g, "mixT")], w=[pk(bk)])
                    evac_half(bk, b, hf)

            dump("mixT_l%d" % l, g["mixT"], [K(g, "mixT")], BF16)
            dump("zs_l%d" % l, g["zs"], [K(g, "zs")])
            barrierA()
            post_norm_residual(0, wo_half)
            dump("x1_l%d" % l, x, [xkey + str(b_) for b_ in range(TB)])

            marks.setdefault("wo", len(P.ops))
            norm_T(g, x, xkey, l, 1)
            for s6 in range(6):
                gslot, gkey = w_next("g")
                uslot, ukey = w_next("u", hold=1)
                noc = 4 if s6 < 5 else 2
                for oc in range(noc):
                    j = s6 * 4 + oc
                    bg = nb()
                    for kc in range(8):
                        E("pe", "matmul", psb[bg][:, 0:T], lhsT=gslot[:, kc, oc * 128:(oc + 1) * 128], rhs=hT[:, kc, :],
                          start=(kc == 0), stop=(kc == 7), r=[gkey, K(g, "hT")], w=[pk(bg)])
                    bu = nb()
                    for kc in range(8):
                        E("pe", "matmul", psb[bu][:, 0:T], lhsT=uslot[:, kc, oc * 128:(oc + 1) * 128], rhs=hT[:, kc, :],
                          start=(kc == 0), stop=(kc == 7), r=[ukey, K(g, "hT")], w=[pk(bu)])
                    jk = K(g, "junk")
                    E("act", "activation", g["junk"][:, 0:T], psb[bg][:, 0:T], AF.Silu, r=[pk(bg)], w=[jk])
                    E("dve", "tensor_tensor", g["actT"][:, j, :], g["junk"][:, 0:T], psb[bu][:, 0:T], ALU.mult,
                      r=[jk, pk(bu)], w=[K(g, "actT")])

            def down_half(hf):
                banks = [4, 5, 6, 7][:TB]
                for q in range(3):
                    slot, skey = w_next("d")
                    nk = 8 if q < 2 else 6
                    for b in range(TB):
                        for kl in range(nk):
                            kc = q * 8 + kl
                            E("pe", "matmul", psb[banks[b]][0:PB, :], lhsT=g["actT"][:, kc, b * PB:(b + 1) * PB], rhs=slot[:, kl, :],
                              start=(kc == 0), stop=(kc == 21), r=[skey, K(g, "actT")], w=[pk(banks[b])])
                for b in range(TB):
                    evac_half(banks[b], b, hf)

            post_norm_residual(1, down_half)
            dump("x2_l%d" % l, x, [xkey + str(b_) for b_ in range(TB)])
            marks.setdefault("tl0", len(P.ops))

    gp = make_group("p", 128, 4, 1, 512, ntp)

    def init_group_zero(g):
        E("dve", "memset", g["halo"][:], 0.0, w=[K(g, "halo0"), K(g, "halo1")])
        E("dve", "memset", g["halosc"][:], 0.0, w=[K(g, "halosc0"), K(g, "halosc1")])
        keys = [K(g, "S%d_%d" % (l, s)) for l in range(DEPTH) for s in range(g["NS"])]
        E("dve", "memset", g["S"][:], 0.0, w=keys)
        E("dve", "memset", g["Sb"][:], 0.0, w=[k + "b" for k in keys])

    init_group_zero(gp)
    gh = make_group("h", 3, 1, 1, 3, 1)
    haloI = view("haloI", F32, 128, [DEPTH, 12, 1, 3])
    haloscI = view("haloscI", F32, 128, [DEPTH, 4, 1, 2])
    tmpCH = view("tmpCH", F32, 128, [4, 3])
    maskt = raws["maskt"]
    xhsave = view("scpre", F32, 3, [DM], off=1024)
    DMA("sp", maskt[:], mask_d, w=["maskt"])
    Sin = view("mixT", F32, 128, [4, 128])
    MT = view("mixT", F32, 128, [4, 128], off=512)
    cand = view("mixT", F32, 128, [4, 128], off=1024)
    Gj = view("A", F32, 128, [2, 4, 128])

    def full_barrier():
        allkeys = set()
        for o in P.ops:
            allkeys.update(o["reads"])
            allkeys.update(o["writes"])
        E("act", "copy", raws["dummy"][:, 2:3], raws["dummy"][:, 0:1], r=[], w=sorted(allkeys, key=str))

    def halo_tile(l, part):
        g = gh
        barrierA()
        norm_T(g, g["xtok"], "xtok", l, 0)
        rr = [0]

        def nbh():
            rr[0] += 1
            return rr[0] % 4

        for s_ in range(3 if part == 0 else 0):
            slot, skey = w_next("in")
            for oc in range(4):
                c = s_ * 4 + oc
                bk = nbh()
                for kc in range(8):
                    E("pe", "matmul", psb[bk][:, 0:3], lhsT=slot[:, kc, oc * 128:(oc + 1) * 128], rhs=g["hT"][:, kc, :],
                      start=(kc == 0), stop=(kc == 7), r=[skey, "hT"], w=[pk(bk)])
                E("act", "copy", haloI[:, l, c, 0, :], psb[bk][:, 0:3], r=[pk(bk)], w=["haloI"])
        if part == 0:
            return
        slot, skey = w_next("in")
        for oc in range(4):
            bk = nbh()
            for kc in range(8):
                E("pe", "matmul", psb[bk][:, 0:3], lhsT=slot[:, kc, oc * 128:(oc + 1) * 128], rhs=g["hT"][:, kc, :],
                  start=(kc == 0), stop=(kc == 7), r=[skey, "hT"], w=[pk(bk)])
            E("act", "copy", tmpCH[:, oc, :], psb[bk][:, 0:3], r=[pk(bk)], w=["tmpCH"])
        slot, skey = w_next("in")
        for oc in range(4):
            bk = nbh()
            for kc in range(8):
                E("pe", "matmul", psb[bk][:, 0:3], lhsT=slot[:, kc, oc * 128:(oc + 1) * 128], rhs=g["hT"][:, kc, :],
                  start=(kc == 0), stop=(kc == 7), r=[skey, "hT"], w=[pk(bk)])
            E("dve", "tensor_tensor", haloscI[:, l, oc, 0, :], tmpCH[:, oc, 1:3], psb[bk][:, 1:3], ALU.mult,
              r=[pk(bk), "tmpCH"], w=["haloscI"])

    xview = xp.rearrange("(t b p) f -> t p b f", b=4, p=128)
    x1view = x1s.rearrange("(t b p) f -> t p b f", b=4, p=128)
    yview = yp.rearrange("(t b p) f -> t p b f", b=4, p=128)
    g = gp
    for l in range(DEPTH):
        src = xview if l == 0 else x1view
        dst = x1view if l == 0 else yview
        if l == 0:
            DMA("sp", gh["xtok"][:, 0, :], xh_in, w=["xtok0"])
        else:
            E("dve", "memset", gh["xtok"][:, 0, :], 0.0, w=["xtok0"])
            for j in range(NRANK):
                DMA("sp", gh["ftmp"][:, 0, :], cc2_dst[3 * j:3 * j + 3, :], r=["cc2_dst"], w=["ftmp"])
                E("dve", "scalar_tensor_tensor", gh["xtok"][:, 0, :], gh["ftmp"][:, 0, :], maskt[0:3, 8 + j:9 + j],
                  gh["xtok"][:, 0, :], ALU.mult, ALU.add, r=["ftmp", "xtok0", "maskt"], w=["xtok0"])
        E("dve", "tensor_copy", xhsave, gh["xtok"][:, 0, :], r=["xtok0"], w=["xhsave"])
        halo_tile(l, 0)
        full_barrier()
        E("dve", "memset", g["Sx"][:, 0], 0.0, w=["Sx"])
        for h in range(4):
            E("dve", "tensor_copy", g["Sx"][:, 1, h, :], ident[:], r=["ident"], w=["Sx"])
        E("act", "copy", g["Sxb"][:], g["Sx"][:], r=["Sx"], w=["Sxb"])
        E("dve", "tensor_copy", g["halo"][:, l], haloI[:, l], r=["haloI"], w=["halo%d" % l])
        def load_x(t, l=l, src=src):
            if t < ntp:
                for b_ in range(4):
                    DMA("act", g["xtok"][:, b_, :], src[t][:, b_, :], r=[("x1s", t, b_)] if l == 1 else [], w=["xtok%d" % b_])

        load_x(0)
        for t in range(ntp):
            tile_layer(g, g["xtok"], "xtok", l, t == 0, False, None, None, None, mode="p1",
                       tt_store=lambda b, l=l, t=t: tts[l, t, b], after_norm=lambda t=t: load_x(t + 1))
        full_barrier()
        DMA("pool", cc_src, g["Sx"].rearrange("p a h c -> p (a h c)"), r=["Sx"], w=["cc_src"])
        P.op("pool", lambda e: e.collective_compute("AllGather", ALU.bypass, replica_groups=[list(range(NRANK))],
                                                    ins=[cc_src.opt()], outs=[cc_dst.opt()]),
             ["cc_src"], ["cc_dst"], cc=True)
        E("dve", "memset", Sin, 0.0, w=["Sin"])
        for j in range(NRANK - 1):
            DMA("sp", Gj.rearrange("p a h c -> p (a h c)"), cc_dst[j * 128:(j + 1) * 128, :], r=["cc_dst"], w=["Gj"])
            for h in range(4):
                E("pe", "transpose", psb[4][:, h * 128:(h + 1) * 128], Gj[:, 1, h, :], ident[:], r=["Gj", "ident"], w=[pk(4)])
            E("act", "copy", MT, psb[4][:, :].rearrange("p (h c) -> p h c", h=4), r=[pk(4)], w=["MT"])
            for h in range(4):
                E("pe", "matmul", psb[5][:, h * 128:(h + 1) * 128], lhsT=MT[:, h, :], rhs=Sin[:, h, :], start=True, stop=True,
                  r=["MT", "Sin"], w=[pk(5)])
            E("dve", "tensor_tensor", cand, psb[5][:, :].rearrange("p (h c) -> p h c", h=4), Gj[:, 0], ALU.add,
              r=[pk(5), "Gj"], w=["cand"])
            E("dve", "tensor_tensor", cand, cand, Sin, ALU.subtract, r=["cand", "Sin"], w=["cand"])
            E("dve", "scalar_tensor_tensor", Sin, cand, maskt[:, j:j + 1], Sin, ALU.mult, ALU.add,
              r=["cand", "Sin", "maskt"], w=["Sin"])
        E("dve", "tensor_copy", g["S"][:, l, 0], Sin, r=["Sin"], w=["S%d_0" % l])
        E("act", "copy", g["Sb"][:, l, 0], Sin, r=["Sin"], w=["S%d_0b" % l])
        full_barrier()
        E("dve", "tensor_copy", gh["xtok"][:, 0, :], xhsave, r=["xhsave"], w=["xtok0"])
        halo_tile(l, 1)
        full_barrier()
        E("dve", "tensor_copy", g["halo"][:, l], haloI[:, l], r=["haloI"], w=["halo%d" % l])
        E("dve", "tensor_copy", g["halosc"][:, l], haloscI[:, l], r=["haloscI"], w=["halosc%d" % l])
        for t in range(ntp):
            load_x(t)
            tile_layer(g, g["xtok"], "xtok", l, t == 0, t == ntp - 1,
                       lambda l, s: convp_o[l], lambda l, s: scp_o[l], lambda l, s: statep_o[l],
                       tt_load=lambda b, l=l, t=t: tts[l, t, b])
            for b_ in range(4):
                DMA("act", dst[t][:, b_, :], g["xtok"][:, b_, :], r=["xtok%d" % b_], w=[("x1s", t, b_)] if l == 0 else [])
            if l == 0 and t == ntp - 1:
                DMA("act", cc2_src, g["xtok"][125:128, 3, :], r=["xtok3"], w=["cc2_src"])
                P.op("pool", lambda e: e.collective_compute("AllGather", ALU.bypass, replica_groups=[list(range(NRANK))],
                                                            ins=[cc2_src.opt()], outs=[cc2_dst.opt()]),
                     ["cc2_src"], ["cc2_dst"], cc=True)
        full_barrier()

    if with_sample:
        g = make_group("s", 16, 2, 2, 16, 1)
        for l in range(DEPTH):
            for s_ in range(2):
                for r3 in range(3):
                    DMA("sp", g["halo"][:, l, :, s_, r3:r3 + 1], cg[l, s_, r3:r3 + 1, :].rearrange("r (c p) -> p c r", p=128),
                        w=["halo%d" % l], allow_slow_non_contiguous=True)
                for r2 in range(2):
                    DMA("sp", g["halosc"][:, l, :, s_, r2:r2 + 1], cs[l, s_, r2:r2 + 1, :].rearrange("r (c p) -> p c r", p=128),
                        w=["halosc%d" % l], allow_slow_non_contiguous=True)
                DMA("sp", g["S"][:, l, s_], stin[l, s_].rearrange("h p v -> p h v"), w=["S%d_%d" % (l, s_)])
                E("act", "copy", g["Sb"][:, l, s_], g["S"][:, l, s_], r=["S%d_%d" % (l, s_)], w=["S%d_%d" % (l, s_) + "b"])
        xb = g["xtok"]
        DMA("sp", xb, xs.rearrange("(b p) f -> p b f", p=16), w=["xtok0", "xtok1"])
        for l in range(DEPTH):
            tile_layer(g, xb, "xtok", l, True, True,
                       lambda l, s: convs_o[l, s], lambda l, s: scs_o[l, s], lambda l, s: states_o[l, s])
        DMA("sp", ys.rearrange("(b p) f -> p b f", p=16), xb, r=["xtok0", "xtok1"])

    marks["tl0"] = marks.get("tl0", len(P.ops))
    if STOP:
        P.ops = P.ops[:marks[STOP]]
        print("STOP at", STOP, len(P.ops))
    print("n_ops", len(P.ops), "sbuf_remaining", nc.sbuf_bytes_remaining)
    P.finalize()
    st.close()
    return nc, P


def host_inputs(core, x_prompt, x_sample, cache_gdn_conv, state_gdn, cache_sc_conv, norm_mix_pre, w_in,
                conv_qkv_w, a_log, dt_bias, gdn_norm_w, conv_sc_w, w_o, norm_mix_post, norm_ffn_pre,
                w_gate, w_up, w_down, norm_ffn_post, shared):
    f = np.float32
    if not shared:
        w_in = np.asarray(w_in, f)
        shared["w_in"] = np.ascontiguousarray(np.concatenate([w_in[:, :, 0:2048], w_in[:, :, 2568:3080],
                                                              w_in[:, :, 3080:3592], w_in[:, :, 2056:2568]], axis=2))
        shared["w_ba"] = np.ascontiguousarray(w_in[:, :, 2048:2056])
        shared["w_o"] = np.ascontiguousarray(np.asarray(w_o, f))
        shared["w_g"] = np.ascontiguousarray(np.asarray(w_gate, f))
        shared["w_u"] = np.ascontiguousarray(np.asarray(w_up, f))
        shared["w_d"] = np.ascontiguousarray(np.asarray(w_down, f))
        gp = np.stack([np.asarray(norm_mix_pre, f), np.asarray(norm_ffn_pre, f)], axis=1)
        shared["gpre"] = np.ascontiguousarray(gp.reshape(DEPTH, 2, 8, 128).transpose(3, 0, 1, 2))
        shared["gpost"] = np.ascontiguousarray(np.stack([np.asarray(norm_mix_post, f), np.asarray(norm_ffn_post, f)], axis=1))
        shared["convw"] = np.ascontiguousarray(np.asarray(conv_qkv_w, f).reshape(DEPTH, 4, 12, 128).transpose(3, 0, 2, 1))
        shared["convsc"] = np.ascontiguousarray(np.asarray(conv_sc_w, f).reshape(DEPTH, 3, 4, 128).transpose(3, 0, 2, 1))
        shared["alog"] = np.ascontiguousarray(np.asarray(a_log, f))
        shared["dtb"] = np.ascontiguousarray(np.asarray(dt_bias, f))
        shared["gnw"] = np.ascontiguousarray(np.asarray(gdn_norm_w, f))
    m = dict(shared)
    return m


NTP_FULL = 32


def kernel(x_prompt, x_sample, cache_gdn_conv, state_gdn, cache_sc_conv, norm_mix_pre, w_in, conv_qkv_w,
           a_log, dt_bias, gdn_norm_w, conv_sc_w, w_o, norm_mix_post, norm_ffn_pre, w_gate, w_up, w_down,
           norm_ffn_post, _ntp=None, _ncores=8, _dbg=False):
    f = np.float32
    x_prompt = np.asarray(x_prompt, f)
    x_sample = np.asarray(x_sample, f)
    cache_gdn_conv = np.asarray(cache_gdn_conv, f)
    state_gdn = np.asarray(state_gdn, f)
    cache_sc_conv = np.asarray(cache_sc_conv, f)
    B, L, _ = x_prompt.shape
    ncores = 8
    nseg = ncores // B
    ntp = _ntp if _ntp is not None else L // (512 * nseg)
    LSEG = ntp * 512
    nc, _ = build(ntp, dbg=_dbg)
    shared = {}
    in_maps = []
    nseq_s = x_sample.shape[0]
    for c in range(ncores):
        m = host_inputs(c, x_prompt, x_sample, cache_gdn_conv, state_gdn, cache_sc_conv, norm_mix_pre, w_in,
                        conv_qkv_w, a_log, dt_bias, gdn_norm_w, conv_sc_w, w_o, norm_mix_post, norm_ffn_pre,
                        w_gate, w_up, w_down, norm_ffn_post, shared)
        sq, sg = c // nseg, c % nseg
        m["xp"] = np.ascontiguousarray(x_prompt[sq, sg * LSEG:(sg + 1) * LSEG])
        xh = np.zeros((3, DM), f)
        if sg > 0:
            xh[:] = x_prompt[sq, sg * LSEG - 3:sg * LSEG]
        m["xh"] = xh
        mk = np.zeros((128, 16), f)
        for j in range(ncores):
            if j // nseg == sq and j < c:
                mk[:, j] = 1.0
            if j // nseg == sq and j == c - 1:
                mk[:, 8 + j] = 1.0
        m["mask"] = mk
        s0 = (2 * c) % nseq_s
        m["xs"] = np.ascontiguousarray(x_sample[s0:s0 + 2].reshape(32, DM))
        m["cg"] = np.ascontiguousarray(cache_gdn_conv[:, s0:s0 + 2])
        m["stin"] = np.ascontiguousarray(state_gdn[:, s0:s0 + 2])
        m["cs"] = np.ascontiguousarray(cache_sc_conv[:, s0:s0 + 2])
        in_maps.append(m)
    res = run_bass_kernel_spmd(nc, in_maps, core_ids=list(range(ncores)))
    R = res.results
    if _dbg:
        kernel.dbg = {k: R[0][k] for k in DBG}
    y_p = np.stack([np.concatenate([R[b * nseg + sg]["yp"] for sg in range(nseg)], axis=0) for b in range(B)], axis=0).astype(f)
    lastc = [b * nseg + nseg - 1 for b in range(B)]
    conv_p = np.stack([R[c]["convp"] for c in lastc], axis=1).astype(f)
    state_p = np.stack([R[c]["statep"] for c in lastc], axis=1).astype(f)
    sc_p = np.stack([R[c]["scp"] for c in lastc], axis=1).astype(f)
    nsc = min(ncores, nseq_s // 2)
    y_s = np.concatenate([R[c]["ys"].reshape(2, 16, DM) for c in range(nsc)], axis=0).astype(f)
    conv_s = np.concatenate([R[c]["convs"] for c in range(nsc)], axis=1).astype(f)
    state_s = np.concatenate([R[c]["states"] for c in range(nsc)], axis=1).astype(f)
    sc_s = np.concatenate([R[c]["scs"] for c in range(nsc)], axis=1).astype(f)
    return (y_p, y_s, conv_p, state_p, sc_p, conv_s, state_s, sc_s)
```

```python
from contextlib import ExitStack
import numpy as np
import concourse.bass as bass
import concourse.mybir as mybir
from concourse.bass_utils import run_bass_kernel_spmd

F32 = mybir.dt.float32
BF16 = mybir.dt.bfloat16
AF = mybir.ActivationFunctionType
ALU = mybir.AluOpType

DM = 1024
DEPTH = 2
DFF = 2816
EPS = 1e-6
BIG = 30000.0


class Prog:
    ENGS = ("pe", "act", "dve", "pool", "sp")

    def __init__(self, nc, n_dma_sems=8):
        self.nc = nc
        self.ops = []
        self.n_dma_sems = n_dma_sems

    def op(self, eng, fn, reads=(), writes=(), dma=False, cc=False):
        self.ops.append(dict(eng=eng, fn=fn, reads=tuple(reads), writes=tuple(writes), dma=dma or cc, cc=cc))

    def finalize(self):
        nc = self.nc
        ops = self.ops
        last_w = {}
        readers = {}
        dma_rr = {e: 0 for e in self.ENGS}
        dma_last = {}
        for i, o in enumerate(ops):
            deps = set()
            for r in o["reads"]:
                if r in last_w:
                    deps.add(last_w[r])
            for w in o["writes"]:
                if w in last_w:
                    deps.add(last_w[w])
                for rd in readers.get(w, ()):
                    deps.add(rd)
            if o["dma"]:
                if o["cc"]:
                    k = ("cc", 0)
                else:
                    k = (o["eng"], dma_rr[o["eng"]] % self.n_dma_sems)
                    dma_rr[o["eng"]] += 1
                o["dsem"] = k
                if k in dma_last:
                    deps.add(dma_last[k])
                dma_last[k] = i
            deps.discard(i)
            if o["eng"] == "pe" and not o["dma"]:
                deps = {d for d in deps if not (ops[d]["eng"] == "pe" and not ops[d]["dma"])}
            best = {}
            keep = set()
            for d in deps:
                if ops[d]["dma"]:
                    keep.add(d)
                else:
                    e_ = ops[d]["eng"]
                    if e_ not in best or best[e_] < d:
                        best[e_] = d
            keep.update(best.values())
            deps = keep
            o["deps"] = deps
            for w in o["writes"]:
                last_w[w] = i
                readers[w] = []
            for r in o["reads"]:
                if r not in o["writes"]:
                    readers.setdefault(r, []).append(i)
        for o in ops:
            o["sig"] = o["dma"]
        for o in ops:
            for d in o["deps"]:
                ops[d]["sig"] = True
        cnt = {e: 0 for e in self.ENGS}
        dcnt = {}
        for o in ops:
            if o["dma"]:
                dcnt[o["dsem"]] = dcnt.get(o["dsem"], 0) + (1 if o["cc"] else 16)
                o["ev"] = (("d",) + o["dsem"], dcnt[o["dsem"]])
            elif o["sig"]:
                cnt[o["eng"]] += 1
                o["ev"] = (("e", o["eng"]), cnt[o["eng"]])
        self.max_counts = dict(cnt)
        with ExitStack() as st:
            sems = {}
            for e in self.ENGS:
                sems[("e", e)] = st.enter_context(nc.semaphore("S_" + e))
            for k in dcnt:
                sems[("d",) + k] = st.enter_context(nc.semaphore("D_%s%d" % k))
            block = st.enter_context(nc.Block())
            final = {}
            for o in ops:
                if o["dma"]:
                    final[o["ev"][0]] = o["ev"][1]

            def run(ename):
                def body(eng):
                    waited = {}
                    for o in ops:
                        if o["eng"] != ename:
                            continue
                        need = {}
                        for d in o["deps"]:
                            s, v = ops[d]["ev"]
                            if need.get(s, 0) < v:
                                need[s] = v
                        for s, v in need.items():
                            if waited.get(s, 0) < v:
                                eng.wait_ge(sems[s], v)
                                waited[s] = v
                        ins = o["fn"](eng)
                        if o["sig"]:
                            ins.then_inc(sems[o["ev"][0]], 16 if (o["dma"] and not o["cc"]) else 1)
                    if ename == "sp":
                        for s, v in final.items():
                            if waited.get(s, 0) < v:
                                eng.wait_ge(sems[s], v)
                return body

            block.tensor(run("pe"))
            block.scalar(run("act"))
            block.vector(run("dve"))
            block.gpsimd(run("pool"))
            block.sync(run("sp"))


def layer_slabs():
    s = []
    for i in range(7):
        s.append(("in", i, 8, 512))
    for i in range(2):
        s.append(("o", i, 8, 512))
    for i in range(6):
        w = 512 if i < 5 else 256
        s.append(("g", i, 8, w))
        s.append(("u", i, 8, w))
    for hf in range(2):
        for q in range(3):
            s.append(("d", hf * 3 + q, 8 if q < 2 else 6, 512))
    return s


import os
DBG = []
STOP = os.environ.get("KSTOP", "")
CFG = {"chain": BF16, "state": BF16}


def build(ntp, with_sample=True, dbg=False):
    nc = bass.Bass("TRN2", target_bir_lowering=False)
    dbg_n = [0]

    def dump(tag, ap, keys, dt=F32):
        if not dbg:
            return
        nm = "dbg%d_%s" % (dbg_n[0], tag)
        dbg_n[0] += 1
        d = nc.dram_tensor(nm, list(ap.shape), dt, kind="ExternalOutput").ap()
        P.op("pool", lambda e: e.dma_start(out=d, in_=ap), keys, (), dma=True)
        DBG.append(nm)

    LP = ntp * 512

    def din(name, shape, dt=F32):
        return nc.dram_tensor(name, list(shape), dt, kind="ExternalInput").ap()

    def dout(name, shape, dt=F32):
        return nc.dram_tensor(name, list(shape), dt, kind="ExternalOutput").ap()

    def dscr(name, shape, dt):
        return nc.dram_tensor(name, list(shape), dt).ap()

    NRANK = 8
    xp = din("xp", [LP, DM])
    xh_in = din("xh", [3, DM])
    mask_d = din("mask", [128, 16])
    x1s = dscr("x1s", [LP, DM], F32)
    cc_src = dscr("cc_src", [128, 1024], F32)
    cc_dst = dscr("cc_dst", [8 * 128, 1024], F32)
    cc2_src = dscr("cc2_src", [3, DM], F32)
    cc2_dst = dscr("cc2_dst", [24, DM], F32)
    tts = dscr("tts", [DEPTH, ntp, 4, 128, 512], BF16)
    xs = din("xs", [32, DM])
    cg = din("cg", [DEPTH, 2, 3, 1536])
    stin = din("stin", [DEPTH, 2, 4, 128, 128])
    cs = din("cs", [DEPTH, 2, 2, 512])
    w_in = din("w_in", [DEPTH, DM, 3584])
    w_ba = din("w_ba", [DEPTH, DM, 8])
    w_o = din("w_o", [DEPTH, DM, DM])
    w_g = din("w_g", [DEPTH, DM, DFF])
    w_u = din("w_u", [DEPTH, DM, DFF])
    w_d = din("w_d", [DEPTH, DFF, DM])
    gpre_d = din("gpre", [128, DEPTH, 2, 8])
    gpost_d = din("gpost", [DEPTH, 2, DM])
    convw_d = din("convw", [128, DEPTH, 12, 4])
    convsc_d = din("convsc", [128, DEPTH, 4, 3])
    alog_d = din("alog", [DEPTH, 4])
    dtb_d = din("dtb", [DEPTH, 4])
    gnw_d = din("gnw", [DEPTH, 128])

    yp = dout("yp", [LP, DM])
    ys = dout("ys", [32, DM])
    convp_o = dout("convp", [DEPTH, 3, 1536])
    statep_o = dout("statep", [DEPTH, 4, 128, 128])
    scp_o = dout("scp", [DEPTH, 2, 512])
    convs_o = dout("convs", [DEPTH, 2, 3, 1536])
    states_o = dout("states", [DEPTH, 2, 4, 128, 128])
    scs_o = dout("scs", [DEPTH, 2, 2, 512])

    sc_in = dscr("sc_in", [DEPTH, 7, 128, 4096], BF16)
    sc_o = dscr("sc_o", [DEPTH, 2, 128, 4096], BF16)
    sc_g = dscr("sc_g", [DEPTH, 6, 128, 4096], BF16)
    sc_u = dscr("sc_u", [DEPTH, 6, 128, 4096], BF16)
    sc_d = dscr("sc_d", [DEPTH, 6, 128, 4096], BF16)
    sc_ba = dscr("sc_ba", [DEPTH, 128, 64], BF16)

    P = Prog(nc)
    st = ExitStack()

    def sb(name, shape, dt=F32):
        return st.enter_context(nc.sbuf_tensor(name, list(shape), dt))

    def E(eng, method, *args, r=(), w=(), **kw):
        P.op(eng, lambda e: getattr(e, method)(*args, **kw), r, w)

    def DMA(q, out, in_, r=(), w=(), **kw):
        P.op(q, lambda e: e.dma_start(out=out, in_=in_, **kw), r, w, dma=True)

    ident = sb("ident", [128, 128])
    identb = sb("identb", [128, 128], BF16)
    Umat = sb("Umat", [128, 128])
    ones = sb("ones", [128, 128])
    onesb = sb("onesb", [128, 128], BF16)
    mstrict = sb("mstrict", [128, 128])
    minclT = sb("minclT", [128, 128])
    E("pool", "memset", ident[:], 1.0, w=["ident"])
    E("pool", "affine_select", out=ident[:], in_=ident[:], pattern=[[-1, 128]], compare_op=ALU.is_equal,
      fill=0.0, base=0, channel_multiplier=1, r=["ident"], w=["ident"])
    E("dve", "tensor_copy", identb[:], ident[:], r=["ident"], w=["identb"])
    E("pool", "memset", Umat[:], 1.0, w=["Umat"])
    E("pool", "affine_select", out=Umat[:], in_=Umat[:], pattern=[[1, 128]], compare_op=ALU.is_ge,
      fill=0.0, base=0, channel_multiplier=-1, r=["Umat"], w=["Umat"])
    E("pool", "memset", ones[:], 1.0, w=["ones"])
    E("pool", "memset", onesb[:], 1.0, w=["onesb"])
    E("pool", "memset", mstrict[:], 0.0, w=["mstrict"])
    E("pool", "affine_select", out=mstrict[:], in_=mstrict[:], pattern=[[-1, 128]], compare_op=ALU.is_gt,
      fill=BIG, base=0, channel_multiplier=1, r=["mstrict"], w=["mstrict"])
    E("pool", "memset", minclT[:], 0.0, w=["minclT"])
    E("pool", "affine_select", out=minclT[:], in_=minclT[:], pattern=[[1, 128]], compare_op=ALU.is_ge,
      fill=BIG, base=0, channel_multiplier=-1, r=["minclT"], w=["minclT"])

    gpre = sb("gpre_s", [128, DEPTH, 2, 8])
    convw = sb("convw_s", [128, DEPTH, 12, 4])
    convsc = sb("convsc_s", [128, DEPTH, 4, 3])
    negA = sb("negA", [128, DEPTH, 4])
    dtb = sb("dtb_s", [128, DEPTH, 4])
    gnw = sb("gnw_s", [128, DEPTH, 128])
    wba = sb("wba_s", [128, DEPTH, 8, 8], BF16)
    DMA("sp", gpre[:], gpre_d, w=["gpre"])
    DMA("sp", convw[:], convw_d, w=["convw"])
    DMA("sp", convsc[:], convsc_d, w=["convsc"])
    DMA("sp", negA[:].rearrange("p l h -> p (l h)"), alog_d.rearrange("l h -> (l h)").partition_broadcast(128), w=["negA"])
    DMA("sp", dtb[:].rearrange("p l h -> p (l h)"), dtb_d.rearrange("l h -> (l h)").partition_broadcast(128), w=["dtb"])
    DMA("sp", gnw[:].rearrange("p l f -> p (l f)"), gnw_d.rearrange("l f -> (l f)").partition_broadcast(128), w=["gnw"])
    E("act", "activation", negA[:], negA[:], AF.Exp, r=["negA"], w=["negA"])
    E("dve", "tensor_scalar", negA[:], negA[:], -1.0, None, ALU.mult, r=["negA"], w=["negA"])

    def cast_slab(dst, src2d, r0, nk, c0, wdt, key):
        d = dst.rearrange("p (k c) -> p k c", k=8)[:, 0:nk, 0:wdt]
        s = src2d[r0:r0 + nk * 128, c0:c0 + wdt].rearrange("(k p) c -> p k c", p=128)
        DMA("pool", d, s, w=[key])

    for l in range(DEPTH):
        for i in range(7):
            cast_slab(sc_in[l, i], w_in[l], 0, 8, i * 512, 512, ("sc", l, "in", i))
        for i in range(2):
            cast_slab(sc_o[l, i], w_o[l], 0, 8, i * 512, 512, ("sc", l, "o", i))
        for i in range(6):
            wdt = 512 if i < 5 else 256
            cast_slab(sc_g[l, i], w_g[l], 0, 8, i * 512, wdt, ("sc", l, "g", i))
            cast_slab(sc_u[l, i], w_u[l], 0, 8, i * 512, wdt, ("sc", l, "u", i))
        for hf in range(2):
            for q in range(3):
                cast_slab(sc_d[l, hf * 3 + q], w_d[l], q * 1024, 8 if q < 2 else 6, hf * 512, 512,
                          ("sc", l, "d", hf * 3 + q))
        DMA("pool", sc_ba[l].rearrange("p (k c) -> p k c", k=8),
            w_ba[l].rearrange("(k p) c -> p k c", p=128), w=[("sc", l, "ba")])
        DMA("sp", wba[:, l], sc_ba[l].rearrange("p (k c) -> p k c", k=8), r=[("sc", l, "ba")], w=["wba"])

    scr = {"in": sc_in, "o": sc_o, "g": sc_g, "u": sc_u, "d": sc_d}
    marks = {}
    marks["prologue"] = len(P.ops)

    NSLOT = 4
    wslots = [sb("wslot%d" % i, [128, 8, 512], BF16) for i in range(NSLOT)]
    LS = layer_slabs()
    SL_P1 = [x for x in LS if x[0] == "in" and x[1] < 3]
    SL_HALO2 = [x for x in LS if x[0] == "in" and x[1] in (4, 5)]
    plan = []
    for l in range(DEPTH):
        plan.append((l, SL_P1))
        for t in range(ntp):
            plan.append((l, SL_P1[1:]))
        plan.append((l, SL_HALO2))
        for t in range(ntp):
            plan.append((l, LS))
    if with_sample:
        for l in range(DEPTH):
            plan.append((l, LS))
    seq = []
    for l, sl in plan:
        for sdesc in sl:
            seq.append((l,) + sdesc)
    wstate = {"issued": 0}

    def w_issue_upto(n):
        while wstate["issued"] <= min(n, len(seq) - 1):
            j = wstate["issued"]
            l, kind, idx, nk, wdt = seq[j]
            slot = wslots[j % NSLOT]
            src = scr[kind][l, idx].rearrange("p (k c) -> p k c", k=8)[:, 0:nk, 0:wdt]
            DMA("sp", slot[:, 0:nk, 0:wdt], src, r=[("sc", l, kind, idx)], w=[("ws", j % NSLOT)])
            wstate["issued"] += 1

    wcur = {"n": 0}

    def w_next(kind, hold=0):
        n = wcur["n"]
        assert seq[n][1] == kind, (seq[n], kind)
        w_issue_upto(n + NSLOT - 1 - hold)
        wcur["n"] += 1
        return wslots[n % NSLOT], ("ws", n % NSLOT)

    psb = [st.enter_context(nc.psum_tensor("ps%d" % i, [128, 512], F32)) for i in range(8)]

    def pk(i):
        return ("ps", i)

    raws = {}

    def raw(name, words):
        raws[name] = sb("raw_" + name, [128, words])
        return raws[name]

    GD_W = 7936 + 1536 + 512 + (2816 if CFG['state'] == F32 else 0)
    for nm, wd in (("xtok", 4096), ("hs", 2048), ("hT", 2048), ("A", 12324), ("scpre", 2056), ("zs", 2048),
                   ("mixT", 2048), ("gpw", 1024), ("halo", 144), ("halosc", 32), ("S", 2048), ("Sb", 2048 if CFG["state"] == F32 else 1024),
                   ("sm", 768), ("smr", 16), ("junk", 512), ("gd", GD_W), ("dummy", 4), ("haloI", 72), ("haloscI", 16),
                   ("tmpCH", 12), ("maskt", 16), ("TTall", 1024)):
        raw(nm, wd)

    def view(rname, dt, parts, shape, off=0):
        flat = raws[rname][:]
        if dt == BF16:
            flat = flat.bitcast(BF16)
        n = int(np.prod(shape))
        v = flat[0:parts, off:off + n]
        if len(shape) == 1:
            return v
        names = "abcde"[:len(shape)]
        pat = "p (%s) -> p %s" % (" ".join(names), " ".join(names))
        kw = {names[i]: shape[i] for i in range(len(shape) - 1)}
        return v.rearrange(pat, **kw)

    def make_group(name, PB, TB, NS, TS, NT):
        T = PB * TB
        g = dict(name=name, PB=PB, TB=TB, NS=NS, TS=TS, NT=NT, T=T)
        g["xtok"] = view("xtok", F32, PB, [TB, DM])
        g["hs"] = view("hs", BF16, PB, [TB, DM])
        g["sq"] = view("hs", BF16, 128, [8, T])
        g["hT"] = view("hT", BF16, 128, [8, T])
        g["pre"] = view("A", F32, 128, [12, NS, TS + 3])
        g["post"] = view("A", F32, 128, [12, T], off=6180)
        g["ftmp"] = view("A", F32, PB, [TB, DM])
        g["actT"] = view("A", BF16, 128, [22, T], off=2 * 6180)
        g["scpre"] = view("scpre", F32, 128, [4, NS, TS + 2])
        g["zs"] = view("zs", F32, PB, [TB, 512])
        g["mixT"] = view("mixT", BF16, 128, [8, T])
        g["gpw"] = view("gpw", F32, PB, [DM])
        g["halo"] = view("halo", F32, 128, [DEPTH, 12, NS, 3])
        g["halosc"] = view("halosc", F32, 128, [DEPTH, 4, NS, 2])
        g["S"] = view("S", F32, 128, [DEPTH, NS, 4, 128])
        g["Sb"] = view("Sb", CFG["state"], 128, [DEPTH, NS, 4, 128])
        g["sm"] = view("sm", F32, PB, [24, TB, 8])
        g["smr"] = view("smr", F32, 128, [TB, 4])
        g["junk"] = view("junk", F32, 128, [512])
        g["TTall"] = view("TTall", BF16, PB, [TB, 4, PB])
        g["Sx"] = view("zs", F32, 128, [2, 4, 128])
        g["Sxb"] = view("zs", BF16, 128, [2, 4, 128], off=2048)
        g["vne"] = view("zs", BF16, PB, [2, 4, 128], off=3072)
        C = PB
        off = [0]

        def gv(dt, parts, shape):
            n = int(np.prod(shape))
            words = n if dt == F32 else (n + 1) // 2
            o = off[0]
            off[0] += words
            return view("gd", dt, parts, shape, off=o if dt == F32 else 2 * o)

        g["knT"] = gv(BF16, 128, [4, C])
        g["qnT"] = gv(BF16, 128, [4, C])
        g["qgT"] = gv(CFG["state"], 128, [4, C])
        g["diag"] = gv(F32, C, [1, 4, C])
        g["egrow"] = gv(F32, 128, [4, C])
        g["dtmp"] = gv(F32, C, [2, 4, C])
        g["Dm"] = gv(F32, C, [2, 4, C])
        g["Pk"] = [gv(CFG["chain"], C, [4, C]) for i in range(2)]
        g["Qk"] = [gv(CFG["chain"], C, [4, C]) for i in range(2)]
        g["TT"] = [gv(CFG["chain"], C, [4, C]) for i in range(2)]
        g["TTs"] = gv(CFG["state"], C, [4, C])
        o_n32 = off[0]
        g["N32"] = gv(F32, C, [4, C])
        g["TT32"] = gv(F32, C, [4, C])
        g["Tnat"] = gv(F32, C, [4, C])
        g["qkT"] = gv(CFG["state"], C, [4, C])
        g["kbeg"] = gv(CFG["state"], C, [4, 128])
        g["ktail"] = gv(CFG["state"], C, [4, 128])
        g["vb"] = gv(CFG["state"], C, [4, 128])
        g["nwT"] = gv(CFG["state"], 128, [4, C])
        g["vnew"] = gv(CFG["state"], C, [4, 128])
        g["og"] = gv(F32, C, [4, 128])
        g["ogb"] = gv(BF16, C, [4, 128])
        g["kbeg2"] = [g["kbeg"], view("gd", BF16, C, [4, 128], off=2 * o_n32)]
        g["ktail2"] = [g["ktail"], view("gd", BF16, C, [4, 128], off=2 * (o_n32 + 256))]
        g["vb2"] = [g["vb"], view("gd", BF16, C, [4, 128], off=2 * (o_n32 + 512))]
        g["nwT2"] = [g["nwT"], view("gd", BF16, 128, [4, C], off=2 * (o_n32 + 768))]
        g["qkT2"] = [g["qkT"], gv(CFG["state"], C, [4, C])]
        g["qgT2"] = [g["qgT"], gv(CFG["state"], 128, [4, C])]
        assert off[0] <= GD_W, off[0]
        if PB == 128 and CFG["chain"] == BF16:
            bw = lambda w_: 2 * w_
            g["cs1"] = dict(Pk=[view("mixT", BF16, C, [4, C], off=bw(0)), view("mixT", BF16, C, [4, C], off=bw(256))],
                            Qk=[view("mixT", BF16, C, [4, C], off=bw(512)), view("mixT", BF16, C, [4, C], off=bw(768))],
                            TT=[view("mixT", BF16, C, [4, C], off=bw(1024)), view("mixT", BF16, C, [4, C], off=bw(1280))],
                            N32=view("mixT", F32, C, [4, C], off=1536),
                            TT32=view("gpw", F32, C, [4, C], off=0), Tnat=view("gpw", F32, C, [4, C], off=512),
                            Tnat2=view("scpre", F32, C, [4, C], off=0), EvK="scpre",
                            TTs=view("TTall", BF16, C, [4, C], off=0), sfx="_1", bP=0, bQ=1, bT=3)
        return g

    def K(g, nm):
        return nm

    AKEYS = ["ftmp", "actT"] + ["post%d" % c for c in range(12)] + ["pre%d" % c for c in range(12)]

    def barrierA():
        d = raws["dummy"]
        E("act", "copy", d[:, 2:3], d[:, 0:1], r=[], w=AKEYS + ["dummy"])

    E("pool", "memset", raws["dummy"][:], 0.0, w=["dummy"])

    def norm_T(g, x, xkey, l, which):
        PB, TB, T = g["PB"], g["TB"], g["T"]
        sm = g["sm"]
        ss = sm[:, 0, :, 0:1]
        rstd = sm[:, 1, :, 0:1]
        E("dve", "memset", sm[:, 0, :, :], 0.0, w=[K(g, "sm0")])
        for b in range(TB):
            E("act", "activation", g["hs"][:, b, :], x[:, b, :], AF.Square, accum_out=sm[:, 0, b, 0:1],
              r=[xkey + str(b), K(g, "sm0")], w=[K(g, "hs"), K(g, "sm0")])
        E("act", "activation", rstd, ss, AF.Ln, bias=EPS, scale=1.0 / DM, r=[K(g, "sm0")], w=[K(g, "sm1")])
        E("act", "activation", rstd, rstd, AF.Exp, scale=-0.5, r=[K(g, "sm1")], w=[K(g, "sm1")])
        for b in range(TB):
            E("dve", "tensor_scalar", g["hs"][:, b, :], x[:, b, :], sm[:, 1, b, 0:1], None, ALU.mult,
              r=[xkey + str(b), K(g, "sm1")], w=[K(g, "hs")])
        for kc in range(8):
            bank = kc % 2
            pv = psb[bank][:].bitcast(BF16)
            for b in range(TB):
                E("pe", "transpose", pv[:, b * PB:(b + 1) * PB], g["hs"][:, b, kc * 128:(kc + 1) * 128],
                  identb[0:PB, 0:PB], r=[K(g, "hs"), "identb"], w=[pk(bank)])
            E("dve", "tensor_scalar", g["hT"][:, kc, :], pv[:, 0:T], gpre[:, l, which, kc:kc + 1], None, ALU.mult,
              r=[pk(bank), "gpre"], w=[K(g, "hT")])

    def tile_layer(g, x, xkey, l, first, last, out_conv, out_sc, out_state, mode="full", tt_store=None, tt_load=None, after_norm=None):
        full = (mode == "full")
        need_o = full
        need_chain = tt_load is None
        if tt_load is not None:
            for b_ in range(g["TB"]):
                DMA("pool", g["TTall"][:, b_].rearrange("p h c -> p (h c)"), tt_load(b_), r=[("tts", b_)], w=["TTall"])
        PB, TB, NS, TS, T = g["PB"], g["TB"], g["NS"], g["TS"], g["T"]
        C = PB
        sm = g["sm"]
        hT, pre, post = g["hT"], g["pre"], g["post"]
        barrierA()
        norm_T(g, x, xkey, l, 0)
        if after_norm is not None:
            after_norm()

        PREK = ["pre%d" % c for c in range(12)]
        E("pool", "tensor_copy", pre[:, :, :, 0:3], g["halo"][:, l], r=[K(g, "halo%d" % l)], w=PREK)

        def conv_chunk(c):
            pv = post[:, c, :].rearrange("p (s t) -> p s t", s=NS)
            E("dve", "tensor_scalar", pv, pre[:, c, :, 0:TS], convw[:, l, c, 0:1], None, ALU.mult,
              r=["pre%d" % c, "convw"], w=[K(g, "post%d" % c)])
            for i in (1, 2, 3):
                E("dve", "scalar_tensor_tensor", pv, pre[:, c, :, i:i + TS], convw[:, l, c, i:i + 1], pv,
                  ALU.mult, ALU.add, r=["pre%d" % c, K(g, "post%d" % c), "convw"], w=[K(g, "post%d" % c)])
            E("act", "activation", post[:, c, :], post[:, c, :], AF.Silu, r=[K(g, "post%d" % c)], w=[K(g, "post%d" % c)])
            if c < 8:
                E("act", "activation", g["sq"][:, c, :], post[:, c, :], AF.Square,
                  r=[K(g, "post%d" % c)], w=[K(g, "hs")])

        early_conv = (lambda s_: True) if not full else (lambda s_: s_ < 2)
        if full:
            E("pool", "tensor_copy", g["scpre"][:, :, :, 0:2], g["halosc"][:, l], r=[K(g, "halosc%d" % l)], w=[K(g, "scpre")])

        marks.setdefault("norm", len(P.ops))
        bank_rr = [0]

        def nb():
            b = bank_rr[0] % 4
            bank_rr[0] += 1
            return b

        for s in range(3):
            if s == 0 and not full:
                continue
            slot, skey = w_next("in")
            for oc in range(4):
                c = s * 4 + oc
                bk = nb()
                for kc in range(8):
                    E("pe", "matmul", psb[bk][:, 0:T], lhsT=slot[:, kc, oc * 128:(oc + 1) * 128], rhs=hT[:, kc, :],
                      start=(kc == 0), stop=(kc == 7), r=[skey, K(g, "hT")], w=[pk(bk)])
                E("act", "copy", pre[:, c, :, 3:3 + TS], psb[bk][:, 0:T].rearrange("p (s t) -> p s t", s=NS),
                  r=[pk(bk)], w=["pre%d" % c])
            if s > 0 and early_conv(s - 1) and (full or s > 1):
                for oc in range(4):
                    conv_chunk((s - 1) * 4 + oc)
        if early_conv(2):
            for oc in range(4):
                conv_chunk(8 + oc)
        if full:
            slot, skey = w_next("in")
            for b in range(TB):
                bk = nb()
                for kc in range(8):
                    E("pe", "matmul", psb[bk][0:PB, :], lhsT=hT[:, kc, b * PB:(b + 1) * PB], rhs=slot[:, kc, :],
                      start=(kc == 0), stop=(kc == 7), r=[skey, K(g, "hT")], w=[pk(bk)])
                E("act", "activation", g["zs"][:, b, :], psb[bk][0:PB, :], AF.Silu, r=[pk(bk)], w=[K(g, "zs")])
        bk = nb()
        for b in range(TB):
            for kc in range(8):
                E("pe", "matmul", psb[bk][0:PB, b * 8:(b + 1) * 8], lhsT=hT[:, kc, b * PB:(b + 1) * PB],
                  rhs=wba[:, l, kc, :], start=(kc == 0), stop=(kc == 7), r=["wba", K(g, "hT")], w=[pk(bk)])
        E("dve", "tensor_copy", sm[:, 2, :, :], psb[bk][0:PB, 0:TB * 8].rearrange("p (b c) -> p b c", b=TB),
          r=[pk(bk)], w=[K(g, "sm2")])
        if full:
            slot, skey = w_next("in")
            for oc in range(4):
                bk = nb()
                for kc in range(8):
                    E("pe", "matmul", psb[bk][:, 0:T], lhsT=slot[:, kc, oc * 128:(oc + 1) * 128], rhs=hT[:, kc, :],
                      start=(kc == 0), stop=(kc == 7), r=[skey, K(g, "hT")], w=[pk(bk)])
                E("act", "copy", g["scpre"][:, oc, :, 2:2 + TS], psb[bk][:, 0:T].rearrange("p (s t) -> p s t", s=NS),
                  r=[pk(bk)], w=[K(g, "scpre")])
            slot, skey = w_next("in")
            for oc in range(4):
                bk = nb()
                for kc in range(8):
                    E("pe", "matmul", psb[bk][:, 0:T], lhsT=slot[:, kc, oc * 128:(oc + 1) * 128], rhs=hT[:, kc, :],
                      start=(kc == 0), stop=(kc == 7), r=[skey, K(g, "hT")], w=[pk(bk)])
                E("dve", "tensor_tensor", g["scpre"][:, oc, :, 2:2 + TS], g["scpre"][:, oc, :, 2:2 + TS],
                  psb[bk][:, 0:T].rearrange("p (s t) -> p s t", s=NS), ALU.mult,
                  r=[pk(bk), K(g, "scpre")], w=[K(g, "scpre")])
            for oc in range(4):
                yv = post[:, 8 + oc, :].rearrange("p (s t) -> p s t", s=NS)
                E("dve", "tensor_scalar", yv, g["scpre"][:, oc, :, 0:TS], convsc[:, l, oc, 0:1], None, ALU.mult,
                  r=[K(g, "scpre"), "convsc"], w=[K(g, "post%d" % (8 + oc))])
                for i in (1, 2):
                    E("dve", "scalar_tensor_tensor", yv, g["scpre"][:, oc, :, i:i + TS], convsc[:, l, oc, i:i + 1], yv,
                      ALU.mult, ALU.add, r=[K(g, "scpre"), K(g, "post%d" % (8 + oc)), "convsc"], w=[K(g, "post%d" % (8 + oc))])
            E("pool", "tensor_copy", g["halosc"][:, l], g["scpre"][:, :, :, TS:TS + 2], r=[K(g, "scpre")], w=[K(g, "halosc%d" % l)])
            if last:
                for s in range(NS):
                    for r2 in range(2):
                        DMA("pool", out_sc(l, s)[r2:r2 + 1, :].rearrange("r (c p) -> p c r", p=128),
                            g["halosc"][:, l, :, s, r2:r2 + 1], r=[K(g, "halosc%d" % l)], allow_slow_non_contiguous=True)
            slot, skey = w_next("in")
            for oc in range(4):
                bk = nb()
                for kc in range(8):
                    E("pe", "matmul", psb[bk][:, 0:T], lhsT=slot[:, kc, oc * 128:(oc + 1) * 128], rhs=hT[:, kc, :],
                      start=(kc == 0), stop=(kc == 7), r=[skey, K(g, "hT")], w=[pk(bk)])
                E("dve", "tensor_tensor", g["mixT"][:, 4 + oc, :], post[:, 8 + oc, :], psb[bk][:, 0:T], ALU.mult,
                  r=[pk(bk), K(g, "post%d" % (8 + oc))], w=[K(g, "mixT")])

        marks.setdefault("win", len(P.ops))
        for c in range(12):
            if not early_conv(c // 4):
                conv_chunk(c)
        E("pool", "tensor_copy", g["halo"][:, l], pre[:, :, :, TS:TS + 3], r=PREK, w=[K(g, "halo%d" % l)])
        if last and full:
            for s in range(NS):
                for r3 in range(3):
                    DMA("pool", out_conv(l, s)[r3:r3 + 1, :].rearrange("r (c p) -> p c r", p=128),
                        g["halo"][:, l, :, s, r3:r3 + 1], r=[K(g, "halo%d" % l)], allow_slow_non_contiguous=True)

        marks.setdefault("conv", len(P.ops))
        bk = nb()
        for b in range(TB):
            for c in range(8):
                E("pe", "matmul", psb[bk][0:PB, b * 8 + c:b * 8 + c + 1], lhsT=g["sq"][:, c, b * PB:(b + 1) * PB],
                  rhs=onesb[:, 0:1], start=True, stop=True, r=[K(g, "hs"), "onesb"], w=[pk(bk)])
        E("act", "activation", sm[:, 3, :, :], psb[bk][0:PB, 0:TB * 8].rearrange("p (b c) -> p b c", b=TB), AF.Ln,
          bias=EPS, scale=1.0, r=[pk(bk)], w=[K(g, "sm3")])
        E("act", "activation", sm[:, 3, :, :], sm[:, 3, :, :], AF.Exp, scale=-0.5, r=[K(g, "sm3")], w=[K(g, "sm3")])
        rq = sm[:, 3, :, 0:4]
        rk = sm[:, 3, :, 4:8]
        beta = sm[:, 4, :, 0:4]
        E("act", "activation", beta, sm[:, 2, :, 0:4], AF.Exp, scale=-1.0, r=[K(g, "sm2")], w=[K(g, "sm4")])
        E("dve", "tensor_scalar", beta, beta, 1.0, None, ALU.add, r=[K(g, "sm4")], w=[K(g, "sm4")])
        E("dve", "reciprocal", beta, beta, r=[K(g, "sm4")], w=[K(g, "sm4")])
        glog = sm[:, 4, :, 4:8]
        E("dve", "tensor_tensor", glog, sm[:, 2, :, 4:8], dtb[0:PB, l, :].unsqueeze(1).to_broadcast([PB, TB, 4]), ALU.add,
          r=[K(g, "sm2"), "dtb"], w=[K(g, "sm4")])
        E("act", "activation", glog, glog, AF.Exp, r=[K(g, "sm4")], w=[K(g, "sm4")])
        E("act", "activation", glog, glog, AF.Ln, bias=1.0, scale=1.0, r=[K(g, "sm4")], w=[K(g, "sm4")])
        E("dve", "tensor_tensor", glog, glog, negA[0:PB, l, :].unsqueeze(1).to_broadcast([PB, TB, 4]), ALU.mult,
          r=[K(g, "sm4"), "negA"], w=[K(g, "sm4")])
        bk = nb()
        for b in range(TB):
            E("pe", "matmul", psb[bk][0:PB, b * 8:b * 8 + 4], lhsT=Umat[0:PB, 0:PB], rhs=sm[:, 4, b, 4:8],
              start=True, stop=True, r=[K(g, "sm4"), "Umat"], w=[pk(bk)])
            E("pe", "matmul", psb[bk][0:PB, b * 8 + 4:b * 8 + 8], lhsT=ones[0:PB, 0:PB], rhs=sm[:, 4, b, 4:8],
              start=True, stop=True, r=[K(g, "sm4"), "ones"], w=[pk(bk)])
            E("pe", "matmul", psb[bk][:, 64 + b * 4:64 + b * 4 + 4], lhsT=ones[0:PB, :], rhs=sm[:, 4, b, 4:8],
              start=True, stop=True, r=[K(g, "sm4"), "ones"], w=[pk(bk)])
        E("dve", "tensor_copy", sm[:, 5, :, :], psb[bk][0:PB, 0:TB * 8].rearrange("p (b c) -> p b c", b=TB),
          r=[pk(bk)], w=[K(g, "sm5")])
        E("act", "activation", g["smr"][:], psb[bk][:, 64:64 + TB * 4].rearrange("p (b c) -> p b c", b=TB), AF.Exp,
          r=[pk(bk)], w=[K(g, "smr")])
        gcum = sm[:, 5, :, 0:4]
        glast = sm[:, 5, :, 4:8]
        eg = sm[:, 6, :, 0:4]
        etail = sm[:, 6, :, 4:8]
        E("act", "activation", eg, gcum, AF.Exp, r=[K(g, "sm5")], w=[K(g, "sm6")])
        E("dve", "tensor_tensor", etail, glast, gcum, ALU.subtract, r=[K(g, "sm5")], w=[K(g, "sm6")])
        E("act", "activation", etail, etail, AF.Exp, r=[K(g, "sm6")], w=[K(g, "sm6")])
        ckbeg = sm[:, 7, :, 0:4]
        cktail = sm[:, 7, :, 4:8]
        E("dve", "tensor_tensor", ckbeg, rk, beta, ALU.mult, r=[K(g, "sm3"), K(g, "sm4")], w=[K(g, "sm7")])
        E("dve", "tensor_tensor", ckbeg, ckbeg, eg, ALU.mult, r=[K(g, "sm7"), K(g, "sm6")], w=[K(g, "sm7")])
        E("dve", "tensor_tensor", cktail, rk, etail, ALU.mult, r=[K(g, "sm3"), K(g, "sm6")], w=[K(g, "sm7")])
        rqs = sm[:, 8, :, 0:4]
        nbeta = sm[:, 8, :, 4:8]
        E("dve", "tensor_scalar", rqs, rq, 128.0 ** -0.5, None, ALU.mult, r=[K(g, "sm3")], w=[K(g, "sm8")])
        E("dve", "tensor_scalar", nbeta, beta, -1.0, None, ALU.mult, r=[K(g, "sm4")], w=[K(g, "sm8")])

        marks.setdefault("scal", len(P.ops))
        S = g["S"]
        Sb = g["Sb"]
        def block(b, cs):
            CK = lambda nm: nm + cs["sfx"]
            pipe = full and (tt_load is not None)
            rb = [0, 1, 2] if pipe else [4, 5, 6]
            qb = 3 if pipe else 5
            qkTb = g["qkT2"][b % 2] if pipe else g["qkT"]
            qgTb = g["qgT2"][b % 2] if pipe else g["qgT"]
            qkK = ("qkT%d" % (b % 2)) if pipe else "qkT"
            qgK = ("qgT%d" % (b % 2)) if pipe else "qgT"
            par = (b % 2) if pipe else 0
            kbegb, ktailb, vbb, nwTb = g["kbeg2"][par], g["ktail2"][par], g["vb2"][par], g["nwT2"][par]
            kbK, ktK, vbK, nwK = ["%s%d" % (n_, par) for n_ in ("kbeg", "ktail", "vb", "nwT")]
            tb = [0, 1, 2] if pipe else [4, 5, 6]
            slot_s = b if NS > 1 else 0
            skS = K(g, "S%d_%d" % (l, slot_s))
            tok = slice(b * PB, (b + 1) * PB)
            R = []
            for vi, (vec, vk) in enumerate(((rk, "sm3"), (rqs, "sm8"), (gcum, "sm5"))):
                if vi == 1 and not need_o:
                    R.append(None)
                    continue
                E("dve", "tensor_tensor", g["diag"][:, 0], ident[0:C, 0:C].unsqueeze(1).to_broadcast([C, 4, C]),
                  vec[:, b, :].unsqueeze(2).to_broadcast([C, 4, C]), ALU.mult, r=["ident", K(g, vk)], w=[K(g, "diag")])
                bk = rb[vi]
                E("pe", "matmul", psb[bk][:, 0:4 * C], lhsT=ones[0:C, :], rhs=g["diag"][:, 0].rearrange("p h c -> p (h c)"),
                  start=True, stop=True, r=["ones", K(g, "diag")], w=[pk(bk)])
                R.append(psb[bk][:, 0:4 * C].rearrange("p (h c) -> p h c", h=4))
                if pipe:
                    yield "s"
            marks.setdefault("r1", len(P.ops))
            kraw = post[:, 4:8, tok]
            qraw = post[:, 0:4, tok]
            E("dve", "tensor_tensor", g["knT"][:], kraw, R[0], ALU.mult, r=[K(g, "post%d" % c) for c in range(4, 8)] + [pk(rb[0])],
              w=[K(g, "knT")])
            if need_o:
                E("dve", "tensor_tensor", g["qnT"][:], qraw, R[1], ALU.mult, r=[K(g, "post%d" % c) for c in range(0, 4)] + [pk(rb[1])],
                  w=[K(g, "qnT")])
                E("act", "activation", g["egrow"][:], R[2], AF.Exp, r=[pk(rb[2])], w=[K(g, "egrow")])
            marks.setdefault("r2", len(P.ops))
            if pipe:
                yield "s"
            Rg = g["diag"][:, 0]
            E("act", "copy", Rg, psb[rb[2]][0:C, 0:4 * C].rearrange("p (h c) -> p h c", h=4), r=[pk(rb[2])], w=[K(g, "diag")])
            if pipe:
                yield "s"
            gc_b = gcum[:, b, :].unsqueeze(2).to_broadcast([C, 4, C])
            if need_chain:
                E("dve", "tensor_tensor", g["dtmp"][:, 0], Rg, gc_b, ALU.subtract, r=[K(g, "diag"), K(g, "sm5")], w=[K(g, "dtmp0")])
                E("dve", "tensor_tensor", g["dtmp"][:, 0], g["dtmp"][:, 0], mstrict[0:C, 0:C].unsqueeze(1).to_broadcast([C, 4, C]),
                  ALU.add, r=[K(g, "dtmp0"), "mstrict"], w=[K(g, "dtmp0")])
                E("act", "activation", g["Dm"][:, 0], g["dtmp"][:, 0], AF.Exp, scale=-1.0, r=[K(g, "dtmp0")], w=[K(g, "Dm0")])
            if need_o:
                E("dve", "tensor_tensor", g["dtmp"][:, 1], gc_b, Rg, ALU.subtract, r=[K(g, "diag"), K(g, "sm5")], w=[K(g, "dtmp1")])
                E("dve", "tensor_tensor", g["dtmp"][:, 1], g["dtmp"][:, 1], minclT[0:C, 0:C].unsqueeze(1).to_broadcast([C, 4, C]),
                  ALU.add, r=[K(g, "dtmp1"), "minclT"], w=[K(g, "dtmp1")])
                E("act", "activation", g["Dm"][:, 1], g["dtmp"][:, 1], AF.Exp, scale=-1.0, r=[K(g, "dtmp1")], w=[K(g, "Dm1")])
                E("dve", "tensor_tensor", qgTb[:], g["qnT"][:], g["egrow"][:], ALU.mult, r=[K(g, "qnT"), K(g, "egrow")],
                  w=[qgK])
            marks.setdefault("g1", len(P.ops))
            if pipe:
                yield "s"
            for h in range(4):
                if need_chain:
                    E("pe", "matmul", psb[4][0:C, h * C:(h + 1) * C], lhsT=g["knT"][:, h, :], rhs=g["knT"][:, h, :],
                      start=True, stop=True, r=[K(g, "knT")], w=[pk(4)])
            for h in range(4):
                if need_o:
                    E("pe", "matmul", psb[qb][0:C, h * C:(h + 1) * C], lhsT=g["knT"][:, h, :], rhs=g["qnT"][:, h, :],
                      start=True, stop=True, r=[K(g, "knT"), K(g, "qnT")], w=[pk(qb)])
            KKv = psb[4][0:C, 0:4 * C].rearrange("p (h c) -> p h c", h=4)
            if pipe:
                yield "s"
            QKv = psb[qb][0:C, 0:4 * C].rearrange("p (h c) -> p h c", h=4)
            if need_o:
                E("dve", "tensor_tensor", qkTb[:], QKv, g["Dm"][:, 1], ALU.mult, r=[pk(qb), K(g, "Dm1")], w=[qkK])
            if pipe:
                yield "prepA"
            if need_chain:
                N32 = cs["N32"]
                E("dve", "tensor_tensor", N32, KKv, g["Dm"][:, 0], ALU.mult,
                  r=[pk(4), K(g, "Dm0")], w=[CK("N32")])
                E("dve", "tensor_tensor", N32, N32, nbeta[:, b, :].unsqueeze(2).to_broadcast([C, 4, C]), ALU.mult,
                  r=[CK("N32"), K(g, "sm8")], w=[CK("N32")])
                E("act", "copy", cs["Pk"][0][:], N32, r=[CK("N32")], w=[CK("Pk0")])
                for h in range(4):
                    E("pe", "transpose", psb[6][0:C, h * C:(h + 1) * C], N32[:, h, :], ident[0:C, 0:C],
                      r=[CK("N32"), "ident"], w=[pk(6)])
                E("act", "copy", cs["Qk"][0][:], psb[6][0:C, 0:4 * C].rearrange("p (h c) -> p h c", h=4), r=[pk(6)], w=[CK("Qk0")])
                marks.setdefault("g2", len(P.ops))
                yield "front"
                nlev = int(np.log2(C))
                idc = identb if CFG["chain"] == BF16 else ident
                TTp = psb[cs["bT"]][0:C, 0:4 * C].rearrange("p (h c) -> p h c", h=4)
                for h in range(4):
                    E("pe", "matmul", psb[cs["bT"]][0:C, h * C:(h + 1) * C], lhsT=idc[0:C, 0:C], rhs=idc[0:C, 0:C],
                      start=True, stop=False, r=["identb", "ident"], w=[pk(cs["bT"])])
                    E("pe", "matmul", psb[cs["bT"]][0:C, h * C:(h + 1) * C], lhsT=idc[0:C, 0:C], rhs=cs["Qk"][0][:, h, :],
                      start=False, stop=True, r=["identb", "ident", CK("Qk0")], w=[pk(cs["bT"])])
                cur = 0
                for lev in range(1, nlev):
                    nxt = 1 - cur
                    E("dve", "tensor_copy", cs["TT"][cur][:], TTp, r=[pk(cs["bT"])], w=[CK("TT%d" % cur)])
                    for h in range(4):
                        E("pe", "matmul", psb[cs["bP"]][0:C, h * C:(h + 1) * C], lhsT=cs["Qk"][cur][:, h, :], rhs=cs["Pk"][cur][:, h, :],
                          start=True, stop=True, r=[CK("Qk%d" % cur), CK("Pk%d" % cur)], w=[pk(cs["bP"])])
                    E("act", "copy", cs["Pk"][nxt][:], psb[cs["bP"]][0:C, 0:4 * C].rearrange("p (h c) -> p h c", h=4),
                      r=[pk(cs["bP"])], w=[CK("Pk%d" % nxt)])
                    if lev < nlev - 1:
                        for h in range(4):
                            E("pe", "matmul", psb[cs["bQ"]][0:C, h * C:(h + 1) * C], lhsT=cs["Pk"][cur][:, h, :], rhs=cs["Qk"][cur][:, h, :],
                              start=True, stop=True, r=[CK("Qk%d" % cur), CK("Pk%d" % cur)], w=[pk(cs["bQ"])])
                        E("dve", "tensor_copy", cs["Qk"][nxt][:], psb[cs["bQ"]][0:C, 0:4 * C].rearrange("p (h c) -> p h c", h=4),
                          r=[pk(cs["bQ"])], w=[CK("Qk%d" % nxt)])
                    for h in range(4):
                        E("pe", "matmul", psb[cs["bT"]][0:C, h * C:(h + 1) * C], lhsT=idc[0:C, 0:C], rhs=cs["TT"][cur][:, h, :],
                          start=True, stop=False, r=["identb", "ident", CK("TT%d" % cur)], w=[pk(cs["bT"])])
                        E("pe", "matmul", psb[cs["bT"]][0:C, h * C:(h + 1) * C], lhsT=cs["Pk"][nxt][:, h, :], rhs=cs["TT"][cur][:, h, :],
                          start=False, stop=True, r=[CK("Pk%d" % nxt), CK("TT%d" % cur)], w=[pk(cs["bT"])])
                    cur = nxt
                    yield "lev"
                if CFG["chain"] == F32:
                    E("dve", "tensor_copy", cs["TTs"][:], TTp, r=[pk(cs["bT"])], w=[CK("TTs")])
                else:
                    TT32, Tn = cs["TT32"], cs["Tnat"]
                    bP, bQ = cs["bP"], cs["bQ"]
                    E("act", "copy", TT32, TTp, r=[pk(cs["bT"])], w=[CK("TT32")])
                    for h in range(4):
                        E("pe", "matmul", psb[bP][0:C, h * C:(h + 1) * C], lhsT=N32[:, h, :], rhs=TT32[:, h, :],
                          start=True, stop=True, r=[CK("N32"), CK("TT32")], w=[pk(bP)])
                    for h in range(4):
                        E("pe", "transpose", psb[bQ][0:C, h * C:(h + 1) * C], TT32[:, h, :], ident[0:C, 0:C],
                          r=[CK("TT32"), "ident"], w=[pk(bQ)])
                    Ev = cs["Tnat2"]
                    E("dve", "tensor_tensor", Ev, psb[bP][0:C, 0:4 * C].rearrange("p (h c) -> p h c", h=4), TT32, ALU.subtract,
                      r=[pk(bP), CK("TT32")], w=[cs["EvK"]])
                    E("dve", "tensor_tensor", Ev, Ev, ident[0:C, 0:C].unsqueeze(1).to_broadcast([C, 4, C]), ALU.add,
                      r=[cs["EvK"], "ident"], w=[cs["EvK"]])
                    E("act", "copy", Tn, psb[bQ][0:C, 0:4 * C].rearrange("p (h c) -> p h c", h=4), r=[pk(bQ)], w=[CK("Tnat")])
                    for h in range(4):
                        E("pe", "matmul", psb[bP][0:C, h * C:(h + 1) * C], lhsT=Tn[:, h, :], rhs=Ev[:, h, :],
                          start=True, stop=True, r=[CK("Tnat"), cs["EvK"]], w=[pk(bP)])
                    E("dve", "tensor_tensor", cs["TTs"][:], psb[bP][0:C, 0:4 * C].rearrange("p (h c) -> p h c", h=4), TT32, ALU.add,
                      r=[pk(bP), CK("TT32")], w=[CK("TTs")])
                yield "chain"
            TTf = cs["TTs"]
            cur = "s"
            if tt_store is not None:
                DMA("pool", tt_store(b), TTf.rearrange("p h c -> p (h c)"), r=[CK("TTs")], w=[("tts", b)])
            if tt_load is not None:
                TTf = g["TTall"][:, b]
                cur = "all"
            marks.setdefault("g3", len(P.ops))
            for h in range(4):
                E("pe", "transpose", psb[tb[0]][0:C, h * 128:(h + 1) * 128], post[:, 4 + h, tok], ident[:],
                  r=[K(g, "post%d" % (4 + h)), "ident"], w=[pk(tb[0])])
            for h in range(4):
                E("pe", "transpose", psb[tb[1]][0:C, h * 128:(h + 1) * 128], post[:, 8 + h, tok], ident[:],
                  r=[K(g, "post%d" % (8 + h)), "ident"], w=[pk(tb[1])])
            ktok = psb[tb[0]][0:C, :].rearrange("p (h c) -> p h c", h=4)
            vtok = psb[tb[1]][0:C, :].rearrange("p (h c) -> p h c", h=4)
            if pipe:
                yield "s"
            for h in range(4):
                E("dve", "tensor_scalar", kbegb[:, h, :], ktok[:, h, :], ckbeg[:, b, h:h + 1], None, ALU.mult,
                  r=[pk(tb[0]), K(g, "sm7")], w=[kbK])
                E("dve", "tensor_scalar", ktailb[:, h, :], ktok[:, h, :], cktail[:, b, h:h + 1], None, ALU.mult,
                  r=[pk(tb[0]), K(g, "sm7")], w=[ktK])
                E("dve", "tensor_scalar", vbb[:, h, :], vtok[:, h, :], beta[:, b, h:h + 1], None, ALU.mult,
                  r=[pk(tb[1]), K(g, "sm4")], w=[vbK])
            if pipe:
                yield "s"
            for h in range(4):
                E("pe", "matmul", psb[tb[2]][:, h * C:(h + 1) * C], lhsT=kbegb[:, h, :], rhs=TTf[:, h, :],
                  start=True, stop=True, r=[kbK, CK("TTs"), "TTall"], w=[pk(tb[2])])
            E("dve", "tensor_scalar", nwTb[:], psb[tb[2]][:, 0:4 * C].rearrange("p (h c) -> p h c", h=4), -1.0, None, ALU.mult,
              r=[pk(tb[2])], w=[nwK])
            if pipe:
                yield "prepB"
            marks.setdefault("g4", len(P.ops))
            if full:
                for h in range(4):
                    E("pe", "matmul", psb[4][0:C, h * 128:(h + 1) * 128], lhsT=TTf[:, h, :], rhs=vbb[:, h, :],
                      start=True, stop=False, r=[CK("TTs"), "TTall", vbK], w=[pk(4)])
                    E("pe", "matmul", psb[4][0:C, h * 128:(h + 1) * 128], lhsT=nwTb[:, h, :], rhs=Sb[:, l, slot_s, h, :],
                      start=False, stop=True, r=[nwK, skS + "b"], w=[pk(4)])
                if pipe:
                    yield "s"
                E("act", "copy", g["vnew"][:], psb[4][0:C, :].rearrange("p (h c) -> p h c", h=4), r=[pk(4)], w=[K(g, "vnew")])
                if pipe:
                    yield "s"
                for h in range(4):
                    E("pe", "matmul", psb[5][0:C, h * 128:(h + 1) * 128], lhsT=qgTb[:, h, :], rhs=Sb[:, l, slot_s, h, :],
                      start=True, stop=False, r=[qgK, skS + "b"], w=[pk(5)])
                    E("pe", "matmul", psb[5][0:C, h * 128:(h + 1) * 128], lhsT=qkTb[:, h, :], rhs=g["vnew"][:, h, :],
                      start=False, stop=True, r=[qkK, K(g, "vnew")], w=[pk(5)])
                for h in range(4):
                    E("pe", "matmul", psb[6][:, h * 128:(h + 1) * 128], lhsT=ktailb[:, h, :], rhs=g["vnew"][:, h, :],
                      start=True, stop=True, r=[ktK, K(g, "vnew")], w=[pk(6)])
                if pipe:
                    yield "s"
                Sv = S[:, l, slot_s]
                E("dve", "tensor_tensor", Sv, Sv, g["smr"][:, b, :].unsqueeze(2).to_broadcast([128, 4, 128]), ALU.mult,
                  r=[skS, K(g, "smr")], w=[skS])
                E("dve", "tensor_tensor", Sv, Sv, psb[6][:, :].rearrange("p (h c) -> p h c", h=4), ALU.add,
                  r=[skS, pk(6)], w=[skS])
                E("act", "copy", Sb[:, l, slot_s], Sv, r=[skS], w=[skS + "b"])
                marks.setdefault("g5", len(P.ops))
                if pipe:
                    yield "s"
                ov = psb[5][0:C, :].rearrange("p (h c) -> p h c", h=4)
                E("dve", "memset", sm[:, 9, b, :], 0.0, w=[K(g, "sm9")])
                for h in range(4):
                    E("act", "activation", g["junk"][0:C, 0:128], psb[5][0:C, h * 128:(h + 1) * 128], AF.Square,
                      accum_out=sm[:, 9, b, h:h + 1], r=[pk(5), K(g, "sm9")], w=[K(g, "junk"), K(g, "sm9")])
                E("act", "activation", sm[:, 9, b, 0:4], sm[:, 9, b, 0:4], AF.Ln, bias=EPS, scale=1.0 / 128, r=[K(g, "sm9")], w=[K(g, "sm9")])
                E("act", "activation", sm[:, 9, b, 0:4], sm[:, 9, b, 0:4], AF.Exp, scale=-0.5, r=[K(g, "sm9")], w=[K(g, "sm9")])
                for h in range(4):
                    E("dve", "tensor_scalar", g["og"][:, h, :], ov[:, h, :], sm[:, 9, b, h:h + 1], None, ALU.mult,
                      r=[pk(5), K(g, "sm9")], w=[K(g, "og")])
                E("dve", "tensor_tensor", g["og"][:], g["og"][:], gnw[0:C, l, :].unsqueeze(1).to_broadcast([C, 4, 128]), ALU.mult,
                  r=[K(g, "og"), "gnw"], w=[K(g, "og")])
                E("dve", "tensor_tensor", g["ogb"][:], g["og"][:], g["zs"][:, b, :].rearrange("p (h c) -> p h c", h=4), ALU.mult,
                  r=[K(g, "og"), K(g, "zs")], w=[K(g, "ogb")])
                if pipe:
                    yield "s"
                pvb = psb[7][:].bitcast(BF16)
                for h in range(4):
                    E("pe", "transpose", pvb[:, h * C:(h + 1) * C], g["ogb"][:, h, :], identb[0:C, 0:C],
                      r=[K(g, "ogb"), "identb"], w=[pk(7)])
                E("dve", "tensor_copy", g["mixT"][:, 0:4, tok], pvb[:, 0:4 * C].rearrange("p (h c) -> p h c", h=4),
                  r=[pk(7)], w=[K(g, "mixT")])
            else:
                Sx, Sxb, vne = g["Sx"], g["Sxb"], g["vne"]
                for h in range(4):
                    E("pe", "matmul", psb[4][0:C, h * 128:(h + 1) * 128], lhsT=TTf[:, h, :], rhs=vbb[:, h, :],
                      start=True, stop=False, r=[CK("TTs"), "TTall", vbK], w=[pk(4)])
                    E("pe", "matmul", psb[4][0:C, h * 128:(h + 1) * 128], lhsT=nwTb[:, h, :], rhs=Sxb[:, 0, h, :],
                      start=False, stop=True, r=[nwK, "Sxb"], w=[pk(4)])
                for h in range(4):
                    E("pe", "matmul", psb[5][0:C, h * 128:(h + 1) * 128], lhsT=nwTb[:, h, :], rhs=Sxb[:, 1, h, :],
                      start=True, stop=True, r=[nwK, "Sxb"], w=[pk(5)])
                E("act", "copy", vne[:, 0], psb[4][0:C, :].rearrange("p (h c) -> p h c", h=4), r=[pk(4)], w=["vne"])
                E("dve", "tensor_copy", vne[:, 1], psb[5][0:C, :].rearrange("p (h c) -> p h c", h=4), r=[pk(5)], w=["vne"])
                for part in range(2):
                    bkx = 6 + part
                    for h in range(4):
                        E("pe", "matmul", psb[bkx][:, h * 128:(h + 1) * 128], lhsT=ktailb[:, h, :], rhs=vne[:, part, h, :],
                          start=True, stop=True, r=[ktK, "vne"], w=[pk(bkx)])
                    E("dve", "tensor_tensor", Sx[:, part], Sx[:, part], g["smr"][:, b, :].unsqueeze(2).to_broadcast([128, 4, 128]),
                      ALU.mult, r=["Sx", K(g, "smr")], w=["Sx"])
                    E("dve", "tensor_tensor", Sx[:, part], Sx[:, part], psb[bkx][:, :].rearrange("p (h c) -> p h c", h=4), ALU.add,
                      r=["Sx", pk(bkx)], w=["Sx"])
                E("act", "copy", Sxb[:], Sx[:], r=["Sx"], w=["Sxb"])
        cs0 = dict(Pk=g["Pk"], Qk=g["Qk"], TT=g["TT"], TTs=g["TTs"], N32=g["N32"], TT32=g["TT32"], Tnat=g["Tnat"],
                   Tnat2=g["dtmp"][:, 0], EvK="dtmp0", sfx="", bP=4, bQ=5, bT=7)
        if mode == "p1" and "cs1" in g and TB % 2 == 0 and need_chain:
            for b0 in range(0, TB, 2):
                ga, gb = block(b0, cs0), block(b0 + 1, g["cs1"])
                next(ga)
                next(gb)
                done = [False, False]
                while not all(done):
                    for i_, gg in enumerate((ga, gb)):
                        if not done[i_]:
                            if next(gg) == "chain":
                                done[i_] = True
                for _ in ga:
                    pass
                for _ in gb:
                    pass
        elif full and tt_load is not None:
            gens = [block(b, cs0) for b in range(TB)]
            for v_ in gens[0]:
                if v_ == "prepB":
                    break
            for b in range(TB):
                gA = gens[b]
                gB = gens[b + 1] if b + 1 < TB else None
                doneA, doneB = False, gB is None
                while not (doneA and doneB):
                    if not doneA:
                        try:
                            next(gA)
                        except StopIteration:
                            doneA = True
                    if not doneB:
                        if next(gB) == "prepB":
                            doneB = True
        else:
            for b in range(TB):
                for _ in block(b, cs0):
                    pass
        if last and full:
            for s in range(NS):
                DMA("pool", out_state(l, s).rearrange("h p v -> p h v"), S[:, l, s], r=[K(g, "S%d_%d" % (l, s))])

        if full:
            marks.setdefault("gdn", len(P.ops))
            def post_norm_residual(which, mm_half):
                E("dve", "memset", sm[:, 10, :, :], 0.0, w=[K(g, "sm10")])
                DMA("act", g["gpw"], gpost_d[l, which].partition_broadcast(PB), w=["gpost"])
                for hf in range(2):
                    mm_half(hf)
                E("dve", "tensor_tensor", sm[:, 10, :, 0:1], sm[:, 10, :, 0:1], sm[:, 10, :, 1:2], ALU.add, r=[K(g, "sm10")], w=[K(g, "sm10")])
                E("act", "activation", sm[:, 10, :, 0:1], sm[:, 10, :, 0:1], AF.Ln, bias=EPS, scale=1.0 / DM, r=[K(g, "sm10")], w=[K(g, "sm10")])
                E("act", "activation", sm[:, 10, :, 0:1], sm[:, 10, :, 0:1], AF.Exp, scale=-0.5, r=[K(g, "sm10")], w=[K(g, "sm10")])
                for b in range(TB):
                    E("dve", "scalar_tensor_tensor", g["ftmp"][:, b, :], g["ftmp"][:, b, :], sm[:, 10, b, 0:1], g["gpw"],
                      ALU.mult, ALU.mult, r=[K(g, "ftmp"), K(g, "sm10"), "gpost"], w=[K(g, "ftmp")])
                    E("dve", "tensor_tensor", x[:, b, :], x[:, b, :], g["ftmp"][:, b, :], ALU.add, r=[K(g, "ftmp"), xkey + str(b)], w=[xkey + str(b)])

            def evac_half(bk, b, hf):
                E("act", "copy", g["ftmp"][:, b, hf * 512:(hf + 1) * 512], psb[bk][0:PB, :], r=[pk(bk)], w=[K(g, "ftmp")])
                E("act", "activation", g["junk"][0:PB, 0:512], psb[bk][0:PB, :], AF.Square, accum_out=sm[:, 10, b, hf:hf + 1],
                  r=[pk(bk), K(g, "sm10")], w=[K(g, "junk"), K(g, "sm10")])

            def wo_half(hf):
                slot, skey = w_next("o")
                for b in range(TB):
                    bk = nb()
                    for kc in range(8):
                        E("pe", "matmul", psb[bk][0:PB, :], lhsT=g["mixT"][:, kc, b * PB:(b + 1) * PB], rhs=slot[:, kc, :],
                          start=(kc == 0), stop=(kc == 7), r=[skey, K(g, "mixT")], w=[pk(bk)])
                    evac_half(bk, b, hf)

            dump("mixT_l%d" % l, g["mixT"], [K(g, "mixT")], BF16)
            dump("zs_l%d" % l, g["zs"], [K(g, "zs")])
            barrierA()
            post_norm_residual(0, wo_half)
            dump("x1_l%d" % l, x, [xkey + str(b_) for b_ in range(TB)])

            marks.setdefault("wo", len(P.ops))
            norm_T(g, x, xkey, l, 1)
            for s6 in range(6):
                gslot, gkey = w_next("g")
                uslot, ukey = w_next("u", hold=1)
                noc = 4 if s6 < 5 else 2
                for oc in range(noc):
                    j = s6 * 4 + oc
                    bg = nb()
                    for kc in range(8):
                        E("pe", "matmul", psb[bg][:, 0:T], lhsT=gslot[:, kc, oc * 128:(oc + 1) * 128], rhs=hT[:, kc, :],
                          start=(kc == 0), stop=(kc == 7), r=[gkey, K(g, "hT")], w=[pk(bg)])
                    bu = nb()
                    for kc in range(8):
                        E("pe", "matmul", psb[bu][:, 0:T], lhsT=uslot[:, kc, oc * 128:(oc + 1) * 128], rhs=hT[:, kc, :],
                          start=(kc == 0), stop=(kc == 7), r=[ukey, K(g, "hT")], w=[pk(bu)])
                    jk = K(g, "junk")
                    E("act", "activation", g["junk"][:, 0:T], psb[bg][:, 0:T], AF.Silu, r=[pk(bg)], w=[jk])
                    E("dve", "tensor_tensor", g["actT"][:, j, :], g["junk"][:, 0:T], psb[bu][:, 0:T], ALU.mult,
                      r=[jk, pk(bu)], w=[K(g, "actT")])

            def down_half(hf):
                banks = [4, 5, 6, 7][:TB]
                for q in range(3):
                    slot, skey = w_next("d")
                    nk = 8 if q < 2 else 6
                    for b in range(TB):
                        for kl in range(nk):
                            kc = q * 8 + kl
                            E("pe", "matmul", psb[banks[b]][0:PB, :], lhsT=g["actT"][:, kc, b * PB:(b + 1) * PB], rhs=slot[:, kl, :],
                              start=(kc == 0), stop=(kc == 21), r=[skey, K(g, "actT")], w=[pk(banks[b])])
                for b in range(TB):
                    evac_half(banks[b], b, hf)

            post_norm_residual(1, down_half)
            dump("x2_l%d" % l, x, [xkey + str(b_) for b_ in range(TB)])
            marks.setdefault("tl0", len(P.ops))

    gp = make_group("p", 128, 4, 1, 512, ntp)

    def init_group_zero(g):
        E("dve", "memset", g["halo"][:], 0.0, w=[K(g, "halo0"), K(g, "halo1")])
        E("dve", "memset", g["halosc"][:], 0.0, w=[K(g, "halosc0"), K(g, "halosc1")])
        keys = [K(g, "S%d_%d" % (l, s)) for l in range(DEPTH) for s in range(g["NS"])]
        E("dve", "memset", g["S"][:], 0.0, w=keys)
        E("dve", "memset", g["Sb"][:], 0.0, w=[k + "b" for k in keys])

    init_group_zero(gp)
    gh = make_group("h", 3, 1, 1, 3, 1)
    haloI = view("haloI", F32, 128, [DEPTH, 12, 1, 3])
    haloscI = view("haloscI", F32, 128, [DEPTH, 4, 1, 2])
    tmpCH = view("tmpCH", F32, 128, [4, 3])
    maskt = raws["maskt"]
    xhsave = view("scpre", F32, 3, [DM], off=1024)
    DMA("sp", maskt[:], mask_d, w=["maskt"])
    Sin = view("mixT", F32, 128, [4, 128])
    MT = view("mixT", F32, 128, [4, 128], off=512)
    cand = view("mixT", F32, 128, [4, 128], off=1024)
    Gj = view("A", F32, 128, [2, 4, 128])

    def full_barrier():
        allkeys = set()
        for o in P.ops:
            allkeys.update(o["reads"])
            allkeys.update(o["writes"])
        E("act", "copy", raws["dummy"][:, 2:3], raws["dummy"][:, 0:1], r=[], w=sorted(allkeys, key=str))

    def halo_tile(l, part):
        g = gh
        barrierA()
        norm_T(g, g["xtok"], "xtok", l, 0)
        rr = [0]

        def nbh():
            rr[0] += 1
            return rr[0] % 4

        for s_ in range(3 if part == 0 else 0):
            slot, skey = w_next("in")
            for oc in range(4):
                c = s_ * 4 + oc
                bk = nbh()
                for kc in range(8):
                    E("pe", "matmul", psb[bk][:, 0:3], lhsT=slot[:, kc, oc * 128:(oc + 1) * 128], rhs=g["hT"][:, kc, :],
                      start=(kc == 0), stop=(kc == 7), r=[skey, "hT"], w=[pk(bk)])
                E("act", "copy", haloI[:, l, c, 0, :], psb[bk][:, 0:3], r=[pk(bk)], w=["haloI"])
        if part == 0:
            return
        slot, skey = w_next("in")
        for oc in range(4):
            bk = nbh()
            for kc in range(8):
                E("pe", "matmul", psb[bk][:, 0:3], lhsT=slot[:, kc, oc * 128:(oc + 1) * 128], rhs=g["hT"][:, kc, :],
                  start=(kc == 0), stop=(kc == 7), r=[skey, "hT"], w=[pk(bk)])
            E("act", "copy", tmpCH[:, oc, :], psb[bk][:, 0:3], r=[pk(bk)], w=["tmpCH"])
        slot, skey = w_next("in")
        for oc in range(4):
            bk = nbh()
            for kc in range(8):
                E("pe", "matmul", psb[bk][:, 0:3], lhsT=slot[:, kc, oc * 128:(oc + 1) * 128], rhs=g["hT"][:, kc, :],
                  start=(kc == 0), stop=(kc == 7), r=[skey, "hT"], w=[pk(bk)])
            E("dve", "tensor_tensor", haloscI[:, l, oc, 0, :], tmpCH[:, oc, 1:3], psb[bk][:, 1:3], ALU.mult,
              r=[pk(bk), "tmpCH"], w=["haloscI"])

    xview = xp.rearrange("(t b p) f -> t p b f", b=4, p=128)
    x1view = x1s.rearrange("(t b p) f -> t p b f", b=4, p=128)
    yview = yp.rearrange("(t b p) f -> t p b f", b=4, p=128)
    g = gp
    for l in range(DEPTH):
        src = xview if l == 0 else x1view
        dst = x1view if l == 0 else yview
        if l == 0:
            DMA("sp", gh["xtok"][:, 0, :], xh_in, w=["xtok0"])
        else:
            E("dve", "memset", gh["xtok"][:, 0, :], 0.0, w=["xtok0"])
            for j in range(NRANK):
                DMA("sp", gh["ftmp"][:, 0, :], cc2_dst[3 * j:3 * j + 3, :], r=["cc2_dst"], w=["ftmp"])
                E("dve", "scalar_tensor_tensor", gh["xtok"][:, 0, :], gh["ftmp"][:, 0, :], maskt[0:3, 8 + j:9 + j],
                  gh["xtok"][:, 0, :], ALU.mult, ALU.add, r=["ftmp", "xtok0", "maskt"], w=["xtok0"])
        E("dve", "tensor_copy", xhsave, gh["xtok"][:, 0, :], r=["xtok0"], w=["xhsave"])
        halo_tile(l, 0)
        full_barrier()
        E("dve", "memset", g["Sx"][:, 0], 0.0, w=["Sx"])
        for h in range(4):
            E("dve", "tensor_copy", g["Sx"][:, 1, h, :], ident[:], r=["ident"], w=["Sx"])
        E("act", "copy", g["Sxb"][:], g["Sx"][:], r=["Sx"], w=["Sxb"])
        E("dve", "tensor_copy", g["halo"][:, l], haloI[:, l], r=["haloI"], w=["halo%d" % l])
        def load_x(t, l=l, src=src):
            if t < ntp:
                for b_ in range(4):
                    DMA("act", g["xtok"][:, b_, :], src[t][:, b_, :], r=[("x1s", t, b_)] if l == 1 else [], w=["xtok%d" % b_])

        load_x(0)
        for t in range(ntp):
            tile_layer(g, g["xtok"], "xtok", l, t == 0, False, None, None, None, mode="p1",
                       tt_store=lambda b, l=l, t=t: tts[l, t, b], after_norm=lambda t=t: load_x(t + 1))
        full_barrier()
        DMA("pool", cc_src, g["Sx"].rearrange("p a h c -> p (a h c)"), r=["Sx"], w=["cc_src"])
        P.op("pool", lambda e: e.collective_compute("AllGather", ALU.bypass, replica_groups=[list(range(NRANK))],
                                                    ins=[cc_src.opt()], outs=[cc_dst.opt()]),
             ["cc_src"], ["cc_dst"], cc=True)
        E("dve", "memset", Sin, 0.0, w=["Sin"])
        for j in range(NRANK - 1):
            DMA("sp", Gj.rearrange("p a h c -> p (a h c)"), cc_dst[j * 128:(j + 1) * 128, :], r=["cc_dst"], w=["Gj"])
            for h in range(4):
                E("pe", "transpose", psb[4][:, h * 128:(h + 1) * 128], Gj[:, 1, h, :], ident[:], r=["Gj", "ident"], w=[pk(4)])
            E("act", "copy", MT, psb[4][:, :].rearrange("p (h c) -> p h c", h=4), r=[pk(4)], w=["MT"])
            for h in range(4):
                E("pe", "matmul", psb[5][:, h * 128:(h + 1) * 128], lhsT=MT[:, h, :], rhs=Sin[:, h, :], start=True, stop=True,
                  r=["MT", "Sin"], w=[pk(5)])
            E("dve", "tensor_tensor", cand, psb[5][:, :].rearrange("p (h c) -> p h c", h=4), Gj[:, 0], ALU.add,
              r=[pk(5), "Gj"], w=["cand"])
            E("dve", "tensor_tensor", cand, cand, Sin, ALU.subtract, r=["cand", "Sin"], w=["cand"])
            E("dve", "scalar_tensor_tensor", Sin, cand, maskt[:, j:j + 1], Sin, ALU.mult, ALU.add,
              r=["cand", "Sin", "maskt"], w=["Sin"])
        E("dve", "tensor_copy", g["S"][:, l, 0], Sin, r=["Sin"], w=["S%d_0" % l])
        E("act", "copy", g["Sb"][:, l, 0], Sin, r=["Sin"], w=["S%d_0b" % l])
        full_barrier()
        E("dve", "tensor_copy", gh["xtok"][:, 0, :], xhsave, r=["xhsave"], w=["xtok0"])
        halo_tile(l, 1)
        full_barrier()
        E("dve", "tensor_copy", g["halo"][:, l], haloI[:, l], r=["haloI"], w=["halo%d" % l])
        E("dve", "tensor_copy", g["halosc"][:, l], haloscI[:, l], r=["haloscI"], w=["halosc%d" % l])
        for t in range(ntp):
            load_x(t)
            tile_layer(g, g["xtok"], "xtok", l, t == 0, t == ntp - 1,
                       lambda l, s: convp_o[l], lambda l, s: scp_o[l], lambda l, s: statep_o[l],
                       tt_load=lambda b, l=l, t=t: tts[l, t, b])
            for b_ in range(4):
                DMA("act", dst[t][:, b_, :], g["xtok"][:, b_, :], r=["xtok%d" % b_], w=[("x1s", t, b_)] if l == 0 else [])
            if l == 0 and t == ntp - 1:
                DMA("act", cc2_src, g["xtok"][125:128, 3, :], r=["xtok3"], w=["cc2_src"])
                P.op("pool", lambda e: e.collective_compute("AllGather", ALU.bypass, replica_groups=[list(range(NRANK))],
                                                            ins=[cc2_src.opt()], outs=[cc2_dst.opt()]),
                     ["cc2_src"], ["cc2_dst"], cc=True)
        full_barrier()

    if with_sample:
        g = make_group("s", 16, 2, 2, 16, 1)
        for l in range(DEPTH):
            for s_ in range(2):
                for r3 in range(3):
                    DMA("sp", g["halo"][:, l, :, s_, r3:r3 + 1], cg[l, s_, r3:r3 + 1, :].rearrange("r (c p) -> p c r", p=128),
                        w=["halo%d" % l], allow_slow_non_contiguous=True)
                for r2 in range(2):
                    DMA("sp", g["halosc"][:, l, :, s_, r2:r2 + 1], cs[l, s_, r2:r2 + 1, :].rearrange("r (c p) -> p c r", p=128),
                        w=["halosc%d" % l], allow_slow_non_contiguous=True)
                DMA("sp", g["S"][:, l, s_], stin[l, s_].rearrange("h p v -> p h v"), w=["S%d_%d" % (l, s_)])
                E("act", "copy", g["Sb"][:, l, s_], g["S"][:, l, s_], r=["S%d_%d" % (l, s_)], w=["S%d_%d" % (l, s_) + "b"])
        xb = g["xtok"]
        DMA("sp", xb, xs.rearrange("(b p) f -> p b f", p=16), w=["xtok0", "xtok1"])
        for l in range(DEPTH):
            tile_layer(g, xb, "xtok", l, True, True,
                       lambda l, s: convs_o[l, s], lambda l, s: scs_o[l, s], lambda l, s: states_o[l, s])
        DMA("sp", ys.rearrange("(b p) f -> p b f", p=16), xb, r=["xtok0", "xtok1"])

    marks["tl0"] = marks.get("tl0", len(P.ops))
    if STOP:
        P.ops = P.ops[:marks[STOP]]
        print("STOP at", STOP, len(P.ops))
    print("n_ops", len(P.ops), "sbuf_remaining", nc.sbuf_bytes_remaining)
    P.finalize()
    st.close()
    return nc, P


def host_inputs(core, x_prompt, x_sample, cache_gdn_conv, state_gdn, cache_sc_conv, norm_mix_pre, w_in,
                conv_qkv_w, a_log, dt_bias, gdn_norm_w, conv_sc_w, w_o, norm_mix_post, norm_ffn_pre,
                w_gate, w_up, w_down, norm_ffn_post, shared):
    f = np.float32
    if not shared:
        w_in = np.asarray(w_in, f)
        shared["w_in"] = np.ascontiguousarray(np.concatenate([w_in[:, :, 0:2048], w_in[:, :, 2568:3080],
                                                              w_in[:, :, 3080:3592], w_in[:, :, 2056:2568]], axis=2))
        shared["w_ba"] = np.ascontiguousarray(w_in[:, :, 2048:2056])
        shared["w_o"] = np.ascontiguousarray(np.asarray(w_o, f))
        shared["w_g"] = np.ascontiguousarray(np.asarray(w_gate, f))
        shared["w_u"] = np.ascontiguousarray(np.asarray(w_up, f))
        shared["w_d"] = np.ascontiguousarray(np.asarray(w_down, f))
        gp = np.stack([np.asarray(norm_mix_pre, f), np.asarray(norm_ffn_pre, f)], axis=1)
        shared["gpre"] = np.ascontiguousarray(gp.reshape(DEPTH, 2, 8, 128).transpose(3, 0, 1, 2))
        shared["gpost"] = np.ascontiguousarray(np.stack([np.asarray(norm_mix_post, f), np.asarray(norm_ffn_post, f)], axis=1))
        shared["convw"] = np.ascontiguousarray(np.asarray(conv_qkv_w, f).reshape(DEPTH, 4, 12, 128).transpose(3, 0, 2, 1))
        shared["convsc"] = np.ascontiguousarray(np.asarray(conv_sc_w, f).reshape(DEPTH, 3, 4, 128).transpose(3, 0, 2, 1))
        shared["alog"] = np.ascontiguousarray(np.asarray(a_log, f))
        shared["dtb"] = np.ascontiguousarray(np.asarray(dt_bias, f))
        shared["gnw"] = np.ascontiguousarray(np.asarray(gdn_norm_w, f))
    m = dict(shared)
    return m


NTP_FULL = 32


def kernel(x_prompt, x_sample, cache_gdn_conv, state_gdn, cache_sc_conv, norm_mix_pre, w_in, conv_qkv_w,
           a_log, dt_bias, gdn_norm_w, conv_sc_w, w_o, norm_mix_post, norm_ffn_pre, w_gate, w_up, w_down,
           norm_ffn_post, _ntp=None, _ncores=8, _dbg=False):
    f = np.float32
    x_prompt = np.asarray(x_prompt, f)
    x_sample = np.asarray(x_sample, f)
    cache_gdn_conv = np.asarray(cache_gdn_conv, f)
    state_gdn = np.asarray(state_gdn, f)
    cache_sc_conv = np.asarray(cache_sc_conv, f)
    B, L, _ = x_prompt.shape
    ncores = 8
    nseg = ncores // B
    ntp = _ntp if _ntp is not None else L // (512 * nseg)
    LSEG = ntp * 512
    nc, _ = build(ntp, dbg=_dbg)
    shared = {}
    in_maps = []
    nseq_s = x_sample.shape[0]
    for c in range(ncores):
        m = host_inputs(c, x_prompt, x_sample, cache_gdn_conv, state_gdn, cache_sc_conv, norm_mix_pre, w_in,
                        conv_qkv_w, a_log, dt_bias, gdn_norm_w, conv_sc_w, w_o, norm_mix_post, norm_ffn_pre,
                        w_gate, w_up, w_down, norm_ffn_post, shared)
        sq, sg = c // nseg, c % nseg
        m["xp"] = np.ascontiguousarray(x_prompt[sq, sg * LSEG:(sg + 1) * LSEG])
        xh = np.zeros((3, DM), f)
        if sg > 0:
            xh[:] = x_prompt[sq, sg * LSEG - 3:sg * LSEG]
        m["xh"] = xh
        mk = np.zeros((128, 16), f)
        for j in range(ncores):
            if j // nseg == sq and j < c:
                mk[:, j] = 1.0
            if j // nseg == sq and j == c - 1:
                mk[:, 8 + j] = 1.0
        m["mask"] = mk
        s0 = (2 * c) % nseq_s
        m["xs"] = np.ascontiguousarray(x_sample[s0:s0 + 2].reshape(32, DM))
        m["cg"] = np.ascontiguousarray(cache_gdn_conv[:, s0:s0 + 2])
        m["stin"] = np.ascontiguousarray(state_gdn[:, s0:s0 + 2])
        m["cs"] = np.ascontiguousarray(cache_sc_conv[:, s0:s0 + 2])
        in_maps.append(m)
    res = run_bass_kernel_spmd(nc, in_maps, core_ids=list(range(ncores)))
    R = res.results
    if _dbg:
        kernel.dbg = {k: R[0][k] for k in DBG}
    y_p = np.stack([np.concatenate([R[b * nseg + sg]["yp"] for sg in range(nseg)], axis=0) for b in range(B)], axis=0).astype(f)
    lastc = [b * nseg + nseg - 1 for b in range(B)]
    conv_p = np.stack([R[c]["convp"] for c in lastc], axis=1).astype(f)
    state_p = np.stack([R[c]["statep"] for c in lastc], axis=1).astype(f)
    sc_p = np.stack([R[c]["scp"] for c in lastc], axis=1).astype(f)
    nsc = min(ncores, nseq_s // 2)
    y_s = np.concatenate([R[c]["ys"].reshape(2, 16, DM) for c in range(nsc)], axis=0).astype(f)
    conv_s = np.concatenate([R[c]["convs"] for c in range(nsc)], axis=1).astype(f)
    state_s = np.concatenate([R[c]["states"] for c in range(nsc)], axis=1).astype(f)
    sc_s = np.concatenate([R[c]["scs"] for c in range(nsc)], axis=1).astype(f)
    return (y_p, y_s, conv_p, state_p, sc_p, conv_s, state_s, sc_s)
```
